# Optimizing a Trainium2 kernel written in Bass

```python
import math
import jax
import jax.numpy as jnp
from jax import lax
import numpy as np

D_MODEL = 1024
BATCH = 8
SEQ = 2048
DEPTH = 1

HEAD_DIM = 64
ATTN_HEADS = D_MODEL // 128
ATTN_WIDTH = ATTN_HEADS * HEAD_DIM
MOBA_BLOCK = 256
MOBA_TOP_K = 3
QUERY_CHUNK = 32
ROPE_THETA = 10000.0
SSM_GROUP_DIM = 16
SSM_GROUPS = D_MODEL // 32
SSM_WIDTH = SSM_GROUPS * SSM_GROUP_DIM
SSM_STATE = 64
DT_MIN = 1e-3
DT_MAX = 1e-1
MIX_WIDTH = ATTN_WIDTH + SSM_WIDTH
IN_PROJ_WIDTH = 4 * ATTN_WIDTH + 2 * SSM_WIDTH
NORM_EPS = 1e-6
NEG_INF = -1e30

kernel_name = "hymba_moba_s5_hybrid_layer"


def rms_norm(x, g):
    xf = x.astype(jnp.float32)
    y = xf * lax.rsqrt(jnp.mean(xf * xf, axis=-1, keepdims=True) + NORM_EPS)
    return (y * g.astype(jnp.float32)).astype(x.dtype)


def rotary(t):
    L, dh = t.shape[2], t.shape[3]
    half = dh // 2
    inv_freq = 1.0 / (ROPE_THETA ** (jnp.arange(half, dtype=jnp.float32) / half))
    ang = jnp.arange(L, dtype=jnp.float32)[:, None] * inv_freq[None, :]
    cos, sin = jnp.cos(ang), jnp.sin(ang)
    tf = t.astype(jnp.float32)
    t1, t2 = tf[..., :half], tf[..., half:]
    out = jnp.concatenate([t1 * cos - t2 * sin, t2 * cos + t1 * sin], axis=-1)
    return out.astype(t.dtype)


def moba_attention(q, k, v):
    Bsz, H, L, dh = q.shape
    nb = -(-L // MOBA_BLOCK)
    pad = nb * MOBA_BLOCK - L
    kb = jnp.pad(k, ((0, 0), (0, 0), (0, pad), (0, 0))).reshape(Bsz, H, nb, MOBA_BLOCK, dh)
    vb = jnp.pad(v, ((0, 0), (0, 0), (0, pad), (0, 0))).reshape(Bsz, H, nb, MOBA_BLOCK, dh)
    k_mean = jnp.mean(kb.astype(jnp.float32), axis=3)
    n_sel = min(MOBA_TOP_K, nb - 1)
    scale = 1.0 / math.sqrt(dh)
    n_chunks = L // QUERY_CHUNK
    q_chunks = q.reshape(Bsz, H, n_chunks, QUERY_CHUNK, dh).transpose(2, 0, 1, 3, 4)
    b_idx = jnp.arange(Bsz)[:, None, None, None]
    h_idx = jnp.arange(H)[None, :, None, None]

    def chunk_fn(args):
        ci, qc = args
        start = ci * QUERY_CHUNK
        q_pos = start + jnp.arange(QUERY_CHUNK)
        own = start // MOBA_BLOCK
        k_own = lax.dynamic_index_in_dim(kb, own, axis=2, keepdims=False)
        v_own = lax.dynamic_index_in_dim(vb, own, axis=2, keepdims=False)
        key_pos = own * MOBA_BLOCK + jnp.arange(MOBA_BLOCK)
        s_own = jnp.einsum('bhqd,bhkd->bhqk', qc, k_own).astype(jnp.float32) * scale
        s_own = jnp.where(key_pos[None, :] <= q_pos[:, None], s_own, NEG_INF)
        if n_sel == 0:
            p_own = jax.nn.softmax(s_own, axis=-1).astype(v.dtype)
            return jnp.einsum('bhqk,bhkd->bhqd', p_own, v_own)
        gate = jnp.einsum('bhqd,bhnd->bhqn', qc.astype(jnp.float32), k_mean)
        gate = jnp.where(jnp.arange(nb) < own, gate, NEG_INF)
        _, idx = lax.top_k(gate, n_sel)
        valid = jnp.arange(n_sel) < own
        k_sel = kb[b_idx, h_idx, idx]
        v_sel = vb[b_idx, h_idx, idx]
        s_sel = jnp.einsum('bhqd,bhqnkd->bhqnk', qc, k_sel).astype(jnp.float32) * scale
        s_sel = jnp.where(valid[:, None], s_sel, NEG_INF)
        s_all = jnp.concatenate([s_sel.reshape(Bsz, H, QUERY_CHUNK, n_sel * MOBA_BLOCK), s_own], axis=-1)
        p = jax.nn.softmax(s_all, axis=-1).astype(v.dtype)
        p_sel = p[..., : n_sel * MOBA_BLOCK].reshape(Bsz, H, QUERY_CHUNK, n_sel, MOBA_BLOCK)
        p_own = p[..., n_sel * MOBA_BLOCK:]
        return (jnp.einsum('bhqnk,bhqnkd->bhqd', p_sel, v_sel)
                + jnp.einsum('bhqk,bhkd->bhqd', p_own, v_own))

    out = lax.map(chunk_fn, (jnp.arange(n_chunks), q_chunks))
    return out.transpose(1, 2, 0, 3, 4).reshape(Bsz, H, L, dh)


def s5_branch(u, lam_re, lam_im, b_re, b_im, c_re, c_im, d_skip, log_dt, w_glu, b_glu):
    Bsz, L, _ = u.shape
    uf = u.astype(jnp.float32).reshape(Bsz, L, SSM_GROUPS, SSM_GROUP_DIM)
    lam = lax.complex(lam_re.astype(jnp.float32), lam_im.astype(jnp.float32))
    dt = jnp.exp(log_dt.astype(jnp.float32))[:, None]
    lam_bar = jnp.exp(lam * dt)
    b_mat = lax.complex(b_re.astype(jnp.float32), b_im.astype(jnp.float32))
    b_bar = ((lam_bar - 1.0) / lam)[..., None] * b_mat
    bu = jnp.einsum('blgh,gph->blgp', uf.astype(jnp.complex64), b_bar)
    a = jnp.broadcast_to(lam_bar, (L, SSM_GROUPS, SSM_STATE))[None]

    def combine(e1, e2):
        a1, s1 = e1
        a2, s2 = e2
        return a1 * a2, a2 * s1 + s2

    _, xs = lax.associative_scan(combine, (a, bu), axis=1)
    y = (jnp.einsum('blgp,ghp->blgh', jnp.real(xs), c_re.astype(jnp.float32))
         - jnp.einsum('blgp,ghp->blgh', jnp.imag(xs), c_im.astype(jnp.float32))
         + d_skip.astype(jnp.float32) * uf)
    y = jax.nn.gelu(y.reshape(Bsz, L, SSM_WIDTH))
    y = y * jax.nn.sigmoid(y @ w_glu.astype(jnp.float32) + b_glu.astype(jnp.float32))
    return y.astype(u.dtype)


def setup_inputs(seed: int = 0) -> dict:
    key = jax.random.key(seed)
    ks = jax.random.split(key, 16)
    f32 = jnp.float32
    G, P, Hg = SSM_GROUPS, SSM_STATE, SSM_GROUP_DIM
    x = jax.random.normal(ks[0], (BATCH, SEQ, D_MODEL), f32)
    norm_gain = 1.0 + 0.01 * jax.random.normal(ks[1], (DEPTH, D_MODEL), f32)
    w_in = jax.random.normal(ks[2], (DEPTH, D_MODEL, IN_PROJ_WIDTH), f32) * D_MODEL ** -0.5
    w_out = jax.random.normal(ks[3], (DEPTH, MIX_WIDTH, D_MODEL), f32) * MIX_WIDTH ** -0.5
    lam_re = -0.5 + 0.01 * jax.random.normal(ks[4], (DEPTH, G, P), f32)
    lam_im = (jnp.pi * jnp.arange(P, dtype=f32))[None, None, :] + 0.01 * jax.random.normal(ks[5], (DEPTH, G, P), f32)
    b_re = jax.random.normal(ks[6], (DEPTH, G, P, Hg), f32) * (2.0 * Hg) ** -0.5
    b_im = jax.random.normal(ks[7], (DEPTH, G, P, Hg), f32) * (2.0 * Hg) ** -0.5
    c_re = jax.random.normal(ks[8], (DEPTH, G, Hg, P), f32) * (2.0 * P) ** -0.5
    c_im = jax.random.normal(ks[9], (DEPTH, G, Hg, P), f32) * (2.0 * P) ** -0.5
    d_skip = jax.random.normal(ks[10], (DEPTH, G, Hg), f32)
    log_dt = jax.random.uniform(ks[11], (DEPTH, G), f32, minval=math.log(DT_MIN), maxval=math.log(DT_MAX))
    w_glu = jax.random.normal(ks[12], (DEPTH, SSM_WIDTH, SSM_WIDTH), f32) * SSM_WIDTH ** -0.5
    b_glu = 0.01 * jax.random.normal(ks[13], (DEPTH, SSM_WIDTH), f32)
    final_gain = 1.0 + 0.01 * jax.random.normal(ks[14], (D_MODEL,), f32)
    return {"x": x, "norm_gain": norm_gain, "w_in": w_in, "w_out": w_out,
            "lam_re": lam_re, "lam_im": lam_im, "b_re": b_re, "b_im": b_im,
            "c_re": c_re, "c_im": c_im, "d_skip": d_skip, "log_dt": log_dt,
            "w_glu": w_glu, "b_glu": b_glu, "final_gain": final_gain}


def reference(x, norm_gain, w_in, w_out, lam_re, lam_im, b_re, b_im, c_re, c_im,
              d_skip, log_dt, w_glu, b_glu, final_gain):
    Bsz, L, _ = x.shape
    A, S = ATTN_WIDTH, SSM_WIDTH

    def to_heads(t):
        return t.reshape(Bsz, L, ATTN_HEADS, HEAD_DIM).transpose(0, 2, 1, 3)

    for layer in range(DEPTH):
        h = rms_norm(x, norm_gain[layer])
        proj = jnp.einsum('bld,de->ble', h, w_in[layer])
        q, k, v, z_attn, u_ssm, z_ssm = jnp.split(
            proj, [A, 2 * A, 3 * A, 4 * A, 4 * A + S], axis=-1)
        o_attn = moba_attention(rotary(to_heads(q)), rotary(to_heads(k)), to_heads(v))
        o_attn = o_attn.transpose(0, 2, 1, 3).reshape(Bsz, L, A) * jax.nn.silu(z_attn)
        o_ssm = s5_branch(u_ssm, lam_re[layer], lam_im[layer], b_re[layer], b_im[layer],
                          c_re[layer], c_im[layer], d_skip[layer], log_dt[layer],
                          w_glu[layer], b_glu[layer]) * jax.nn.silu(z_ssm)
        mixed = jnp.concatenate([o_attn, o_ssm], axis=-1)
        x = x + jnp.einsum('ble,ed->bld', mixed, w_out[layer])
    return rms_norm(x, final_gain)
```

```python
import math
from contextlib import ExitStack
import numpy as np
import concourse.bass as bass
import concourse.mybir as mybir
from concourse.bass_utils import run_bass_kernel_spmd

F32 = mybir.dt.float32
BF16 = mybir.dt.bfloat16
ALU = mybir.AluOpType
AF = mybir.ActivationFunctionType
AX = mybir.AxisListType

L = 2048
D = 1024
NEG = -30000.0
EPS = 1e-6


SAME_ENGINE_IN_ORDER = ("pe",)


class _Eng:
    def __init__(self, name, sem):
        self.name = name
        self.sem = sem
        self.count = 0
        self.ops = []
        self.waited = {}


class _Batch:
    def __init__(self, sem, ordered=False):
        self.sem = sem
        self.n = 0
        self.ordered = ordered


class Prog:
    def __init__(self, nc, ctx):
        self.nc = nc
        self.ctx = ctx
        self.engs = {}
        for name in ("pe", "act", "dve", "pool", "sp"):
            sem = ctx.enter_context(nc.semaphore("s_" + name))
            self.engs[name] = _Eng(name, sem)
        self.res = {}
        self.batches = []
        self.stopped = False

    def _st(self, key):
        st = self.res.get(key)
        if st is None:
            st = {"w": None, "r": []}
            self.res[key] = st
        return st

    def _add_waits(self, eng, deps):
        waits = []
        e = self.engs[eng]
        for d in deps:
            if d[0] == "e":
                _, en, idx = d
                if en == eng and eng in SAME_ENGINE_IN_ORDER:
                    continue
                if e.waited.get(en, 0) >= idx:
                    continue
                e.waited[en] = idx
                waits = [w for w in waits if not (w[0] == "e" and w[1] == en)]
                waits.append(d)
            else:
                b = d[1]
                k = d[2] if b.ordered else 1 << 30
                if e.waited.get(id(b), 0) >= k:
                    continue
                e.waited[id(b)] = k
                waits = [w for w in waits if not (w[0] == "b" and w[1] is b)]
                waits.append(d)
        return waits

    def _deps(self, eng, reads, writes):
        deps = []
        for k in reads:
            st = self._st(k)
            if st["w"] is not None:
                deps.append(st["w"])
        for k in writes:
            st = self._st(k)
            if st["w"] is not None:
                deps.append(st["w"])
            deps.extend(st["r"])
        return self._add_waits(eng, deps)

    def _update(self, ev, reads, writes):
        for k in reads:
            self._st(k)["r"].append(ev)
        for k in writes:
            st = self._st(k)
            st["w"] = ev
            st["r"] = []

    def op(self, eng, fn, reads=(), writes=(), signal=True):
        if self.stopped:
            return
        e = self.engs[eng]
        waits = self._deps(eng, reads, writes)
        ev = ("e", eng, e.count + 1)
        if signal:
            e.count += 1
        e.ops.append((waits, fn, (e.sem, 1) if signal else None))
        self._update(ev, reads, writes)

    def new_batch(self, ordered=False):
        sem = self.ctx.enter_context(self.nc.semaphore("s_dma%d" % len(self.batches)))
        b = _Batch(sem, ordered)
        self.batches.append(b)
        return b

    def dma(self, eng, batch, out, in_, reads=(), writes=()):
        if self.stopped:
            return
        e = self.engs[eng]
        waits = self._deps(eng, reads, writes)
        batch.n += 1
        ev = ("b", batch, batch.n)

        def fn(h, out=out, in_=in_):
            return h.dma_start(out=out, in_=in_)
        e.ops.append((waits, fn, (batch.sem, 16)))
        self._update(ev, reads, writes)

    def barrier(self):
        for name, e in self.engs.items():
            deps = [("e", en, o.count) for en, o in self.engs.items() if en != name and o.count > 0]
            deps += [("b", b, b.n) for b in self.batches if b.n > 0]
            waits = self._add_waits(name, deps)
            e.ops.append((waits, None, None))
        self.res = {}

    def emit(self):
        nc = self.nc
        with nc.Block() as block:
            def replay(name, h):
                e = self.engs[name]
                for waits, fn, inc in e.ops:
                    for w in waits:
                        if w[0] == "e":
                            h.wait_ge(self.engs[w[1]].sem, w[2])
                        else:
                            h.wait_ge(w[1].sem, 16 * (w[2] if w[1].ordered else w[1].n))
                    if fn is None:
                        continue
                    ins = fn(h)
                    if inc is not None:
                        ins.then_inc(inc[0], inc[1])

            @block.tensor
            def _(h):
                replay("pe", h)

            @block.scalar
            def _(h):
                replay("act", h)

            @block.vector
            def _(h):
                replay("dve", h)

            @block.gpsimd
            def _(h):
                replay("pool", h)

            @block.sync
            def _(h):
                replay("sp", h)


C_ID = 0
C_TRI = 128
C_G = 256
C_BG = 264
C_BM = 268
C_GV = 276
C_GM = 340
C_PERM = 404
NCST = 532
NPA = 265
NPB = 258


class _Stop(Exception):
    pass


def build(dbg=None, upto=None):
    dbg = dbg or {}
    nc = bass.Bass("TRN2", target_bir_lowering=False)

    def din(name, shape):
        return nc.dram_tensor(name, shape, F32, kind="ExternalInput").ap()
    x_d = din("x", [L, D])
    win_d = din("w_in_c", [12, D, 256])
    wout_d = din("w_out", [D, D])
    wglu_d = din("w_glu", [512, 512])
    cst_d = din("cst", [128, NCST])
    rope_d = din("rope", [128, 2, L])
    fg_d = din("fgain", [1, D])
    oh_d = din("onehot", [8, L])
    pa_d = din("pA", [4, 128, NPA])
    pb_d = din("pB", [4, 128, NPB])
    out_d = nc.dram_tensor("out", [L, D], F32, kind="ExternalOutput").ap()
    dbg_d = {k: nc.dram_tensor("dbg_" + k, [128, L], F32, kind="ExternalOutput").ap() for k, s in dbg.items()}

    with ExitStack() as ctx:
        P = Prog(nc, ctx)

        def sbt(c, name, shape, dt=F32):
            return c.enter_context(nc.sbuf_tensor("sb_" + name, shape, dt))

        ps = [ctx.enter_context(nc.psum_tensor("ps%d" % i, [128, 512], F32)) for i in range(8)]
        PK = ["ps%d" % i for i in range(8)]

        cst = sbt(ctx, "cst", [128, NCST])
        identb = sbt(ctx, "identb", [128, 128], BF16)
        trib = sbt(ctx, "trib", [128, 128], BF16)
        onesf = sbt(ctx, "onesf", [128, 128])
        halfpi = sbt(ctx, "halfpi", [128, 1])
        za = [sbt(ctx, "za%d" % i, [128, L], BF16) for i in range(4)]
        zs = [sbt(ctx, "zs%d" % i, [128, L], BF16) for i in range(4)]
        up = [sbt(ctx, "up%d" % i, [128, 8, 256], BF16) for i in range(4)]
        dump_sb = sbt(ctx, "dump_sb", [128, L]) if dbg else None

        ident = cst[:, C_ID:C_ID + 128]
        gcol = cst[:, C_G:C_G + 8]
        bglu = cst[:, C_BG:C_BG + 4]
        bmask = cst[:, C_BM:C_BM + 8]

        b0 = P.new_batch()
        P.dma("sp", b0, cst[:], cst_d, writes=["cst"])
        P.op("dve", lambda h: h.tensor_copy(identb[:], ident), reads=["cst"], writes=["identb"])
        P.op("dve", lambda h: h.tensor_copy(trib[:], cst[:, C_TRI:C_TRI + 128]), reads=["cst"], writes=["trib"])
        P.op("dve", lambda h: h.memset(onesf[:], 1.0), writes=["onesf"])
        P.op("dve", lambda h: h.memset(halfpi[:], math.pi / 2), writes=["halfpi"])

        dbgb = [P.new_batch(ordered=True) if dbg else None]

        def dump(name, src_ap, key, rows=128, cols=L, row0=0, col0=0, view=None):
            if name not in dbg_d:
                return
            dv = dump_sb[0:rows, 0:cols]
            if view is not None:
                dv = dv.rearrange(view[0], **view[1])
            P.op("dve", lambda h: h.tensor_copy(dv, src_ap), reads=[key], writes=["dump_sb"])
            P.dma("sp", dbgb[0], dbg_d[name], dump_sb[:], reads=["dump_sb"], writes=["DBGOUT"])

        try:
            qkv = ctx.enter_context(ExitStack())
            qT = [sbt(qkv, "qT%d" % i, [128, L], BF16) for i in range(4)]
            kT = [sbt(qkv, "kT%d" % i, [128, L], BF16) for i in range(4)]
            vaug = sbt(qkv, "vaug", [128, 16, 4, 192], BF16)
            P.op("dve", lambda h: h.memset(vaug[:], 0.0), writes=["vaug"])
            P.op("dve", lambda h: h.memset(vaug[:, :, :, 64:65], 1.0), writes=["vaug"])

            with ExitStack() as p1:
                xs = [sbt(p1, "xs%d" % i, [128, D]) for i in range(3)]
                xbf = [sbt(p1, "xbf%d" % i, [128, D], BF16) for i in range(3)]
                junk = sbt(p1, "junk", [128, D], BF16)
                hT = sbt(p1, "hT", [128, 8, L], BF16)
                wst = [sbt(p1, "wst%d" % i, [128, 8, 256]) for i in range(2)]
                wbf = [sbt(p1, "wbf%d" % i, [128, 8, 256], BF16) for i in range(2)]
                ropef = sbt(p1, "ropef", [128, 512])
                rope = sbt(p1, "rope", [128, 2, L], BF16)
                ss = sbt(p1, "ss", [128, 16])
                rst = sbt(p1, "rst", [128, 16])
                Dt = [sbt(p1, "Dt%d" % i, [128, 128]) for i in range(2)]
                rbc = [sbt(p1, "rbc%d" % i, [128, 128]) for i in range(2)]
                t1 = sbt(p1, "t1", [128, 512])
                t2 = sbt(p1, "t2", [128, 512])
                t3 = sbt(p1, "t3", [128, 512])
                permb = sbt(p1, "permb", [128, 128], BF16)
                qbf = [sbt(p1, "qbf%d" % i, [128, 512], BF16) for i in range(2)]
                P.op("dve", lambda h: h.tensor_copy(permb[:], cst[:, C_PERM:C_PERM + 128]), reads=["cst"], writes=["permb"])

                P.op("dve", lambda h: h.memset(ss[:], 0.0), writes=[("ss", i_) for i_ in range(16)])
                brope = P.new_batch()
                bw = [P.new_batch() for _ in range(12)]
                bx = [P.new_batch() for _ in range(16)]

                ropef2 = [ropef, sbt(p1, "ropef_b", [128, 512])]

                def load_rope():
                    n = 0
                    for tch in range(4):
                        for cs in range(2):
                            br = P.new_batch()
                            stg = ropef2[n % 2]
                            P.dma("sp", br, stg[:], rope_d[:, cs, tch * 512:(tch + 1) * 512], writes=[("ropef", n % 2)])
                            P.op("dve", lambda h, cs=cs, tch=tch, stg=stg: h.tensor_copy(rope[:, cs, tch * 512:(tch + 1) * 512], stg[:]),
                                 reads=[("ropef", n % 2)], writes=["rope"])
                            n += 1

                def load_w(c):
                    P.dma("sp", bw[c], wst[c % 2][:], win_d[c].rearrange("(kt p) c -> p kt c", p=128),
                          writes=[("wst", c % 2)])

                load_w(0)
                load_w(1)
                def st_a(i):
                    s2, s1 = i % 3, i % 2
                    P.dma("sp", bx[i], xs[s2][:], x_d[i * 128:(i + 1) * 128, :], writes=[("xs", s2)])
                    P.op("act", lambda h: h.activation(junk[:], xs[s2][:], AF.Square, accum_out=ss[:, i:i + 1]),
                         reads=[("xs", s2)], writes=["junk", ("ss", i)])
                    P.op("act", lambda h: h.copy(xbf[s2][:], xs[s2][:]), reads=[("xs", s2)], writes=[("xbf", s2)])
                    trv = ps[s1][:].bitcast(BF16)
                    for kt in range(8):
                        P.op("pe", lambda h, kt=kt: h.transpose(trv[:, kt * 128:(kt + 1) * 128], xbf[s2][:, kt * 128:(kt + 1) * 128], identb[:]),
                             reads=[("xbf", s2), "identb"], writes=[PK[s1]], signal=(kt == 7))

                def st_b(i):
                    s1 = i % 2
                    trv = ps[s1][:].bitcast(BF16)
                    P.op("dve", lambda h: h.tensor_scalar(rst[:, i:i + 1], ss[:, i:i + 1], 1.0 / D, EPS, ALU.mult, ALU.add),
                         reads=[("ss", i)], writes=[("rst", i)])
                    P.op("act", lambda h: h.activation(rst[:, i:i + 1], rst[:, i:i + 1], AF.Sqrt), reads=[("rst", i)], writes=[("rst", i)])
                    P.op("dve", lambda h: h.reciprocal(rst[:, i:i + 1], rst[:, i:i + 1]), reads=[("rst", i)], writes=[("rst", i)])
                    P.op("dve", lambda h: h.tensor_scalar(Dt[s1][:], ident, rst[:, i:i + 1], None, ALU.mult),
                         reads=[("rst", i), "cst"], writes=[("Dt", s1)])
                    P.op("pe", lambda h: h.matmul(ps[2 + s1][:, 0:128], onesf[:], Dt[s1][:], start=True, stop=True),
                         reads=[("Dt", s1), "onesf"], writes=[PK[2 + s1]])
                    P.op("act", lambda h: h.copy(rbc[s1][:], ps[2 + s1][:, 0:128]), reads=[PK[2 + s1]], writes=[("rbc", s1)])
                    P.op("dve", lambda h: h.tensor_tensor(
                        hT[:, :, i * 128:(i + 1) * 128], trv.rearrange("p (k t) -> p k t", k=8),
                        rbc[s1][:].unsqueeze(1).to_broadcast([128, 8, 128]), ALU.mult),
                        reads=[PK[s1], ("rbc", s1)], writes=["hT"])

                for i in range(17):
                    if i < 16:
                        st_a(i)
                    if i >= 1:
                        st_b(i - 1)
                if upto == "x16":
                    P.stopped = True
                if upto == "xr":
                    P.stopped = True
                dump("xs", xs[1][:], ("xs", 1), cols=1024)
                dump("ss", ss[:], ("ss", 15), cols=16)
                dump("rst", rst[:], ("rst", 15), cols=16)
                dump("rbc", rbc[1][:], ("rbc", 1), cols=128)
                dump("xbf", xbf[1][:], ("xbf", 1), cols=1024)
                dump("Dt", Dt[1][:], ("Dt", 1), cols=128)
                dump("hT0", hT[:, 0, :], "hT")
                if upto == "p1a":
                    P.stopped = True

                def cast_chunk(c):
                    sl = c % 2
                    for kt in range(8):
                        if kt % 2 == 0:
                            P.op("act", lambda h, kt=kt: h.activation(wbf[sl][:, kt, :], wst[sl][:, kt, :], AF.Copy, scale=gcol[:, kt:kt + 1]),
                                 reads=[("wst", sl), "cst"], writes=[("wbf", sl)])
                        else:
                            P.op("dve", lambda h, kt=kt: h.tensor_scalar(wbf[sl][:, kt, :], wst[sl][:, kt, :], gcol[:, kt:kt + 1], None, ALU.mult),
                                 reads=[("wst", sl), "cst"], writes=[("wbf", sl)])
                    if c == 0:
                        load_rope()
                    if c + 2 < 12:
                        load_w(c + 2)

                pend_qk = []
                for c in range(12):
                    sl = c % 2
                    if upto is not None and upto.startswith('c') and upto[1:].isdigit() and c == int(upto[1:]):
                        P.stopped = True
                    if c == 0:
                        cast_chunk(0)
                    if c < 10:
                        for tch in range(4):
                            if tch == 1 and c + 1 < 12:
                                cast_chunk(c + 1)
                            tsl = slice(tch * 512, (tch + 1) * 512)
                            bk = [4 + 2 * (tch % 2), 5 + 2 * (tch % 2)]
                            for g in range(2):
                                for kt in range(8):
                                    P.op("pe", lambda h, g=g, kt=kt, sl=sl, tsl=tsl, bk=bk: h.matmul(
                                        ps[bk[g]][:], wbf[sl][:, kt, g * 128:(g + 1) * 128], hT[:, kt, tsl],
                                        start=(kt == 0), stop=(kt == 7)),
                                        reads=[("wbf", sl), "hT"], writes=[PK[bk[g]]], signal=(kt == 7))
                            if c < 4:
                                def qk_evac(tsl=tsl, bk=bk, c=c):
                                    for g in range(2):
                                        hp_ = 2 * (c % 2) + g
                                        dst = qT[hp_] if c < 2 else kT[hp_]
                                        dk = ("q", hp_) if c < 2 else ("k", hp_)
                                        ta, tb = (t1, t2) if g == 0 else (t3, t2)
                                        tak = "t1" if g == 0 else "t3"
                                        P.op("act", lambda h, g=g, bk=bk: h.copy(qbf[g][:], ps[bk[g]][:]), reads=[PK[bk[g]]], writes=[("qbf", g)])
                                        P.op("pe", lambda h, g=g: h.matmul(ps[2 + g][:], permb[:], qbf[g][:], start=True, stop=True),
                                             reads=["permb", ("qbf", g)], writes=[PK[2 + g]])
                                        P.op("dve", lambda h, g=g, tsl=tsl, bk=bk, ta=ta: h.tensor_tensor(ta[:], ps[bk[g]][:], rope[:, 0, tsl], ALU.mult),
                                             reads=[PK[bk[g]], "rope", ("qbf", g)], writes=[tak])
                                        P.op("dve", lambda h, g=g, tsl=tsl: h.tensor_tensor(t2[:], ps[2 + g][:], rope[:, 1, tsl], ALU.mult),
                                             reads=[PK[2 + g], "rope"], writes=["t2"])
                                        P.op("dve", lambda h, dst=dst, tsl=tsl, ta=ta: h.tensor_tensor(dst[:, tsl], ta[:], t2[:], ALU.add),
                                             reads=[tak, "t2"], writes=[dk])
                                if pend_qk:
                                    pend_qk.pop()()
                                pend_qk.append(qk_evac)
                                if tch == 3:
                                    pend_qk.pop()()
                            elif c in (4, 5, 8, 9):
                                for g in range(2):
                                    dst = za[2 * (c - 4) + g] if c < 6 else zs[2 * (c - 8) + g]
                                    dk = ("za", 2 * (c - 4) + g) if c < 6 else ("zs", 2 * (c - 8) + g)
                                    tt = t1 if g == 0 else t2
                                    tk = "t1" if g == 0 else "t2"
                                    P.op("act", lambda h, g=g, bk=bk, tt=tt: h.activation(tt[:], ps[bk[g]][:], AF.Sigmoid),
                                         reads=[PK[bk[g]]], writes=[tk])
                                    P.op("dve", lambda h, g=g, bk=bk, tt=tt, dst=dst, tsl=tsl: h.tensor_tensor(dst[:, tsl], ps[bk[g]][:], tt[:], ALU.mult),
                                         reads=[PK[bk[g]], tk], writes=[dk])
                            else:
                                for g in range(2):
                                    ct = 2 * (c - 6) + g
                                    dstv = up[ct][:].rearrange("p s j -> p j s")[:, tch * 64:(tch + 1) * 64, :]
                                    P.op("act", lambda h, g=g, bk=bk, dstv=dstv: h.copy(dstv, ps[bk[g]][:].rearrange("p (j s) -> p j s", s=8)),
                                         reads=[PK[bk[g]]], writes=[("up", ct)])
                    else:
                        pr0 = 2 * (c - 10)
                        for i in range(16):
                            if i == 4 and c + 1 < 12:
                                cast_chunk(c + 1)
                            bkv = 4 + (i % 4)
                            for kt in range(8):
                                P.op("pe", lambda h, i=i, kt=kt, sl=sl, bkv=bkv: h.matmul(
                                    ps[bkv][:, 0:256], hT[:, kt, i * 128:(i + 1) * 128], wbf[sl][:, kt, :],
                                    start=(kt == 0), stop=(kt == 7)),
                                    reads=[("wbf", sl), "hT"], writes=[PK[bkv]], signal=(kt == 7))
                            pv = ps[bkv][:, 0:256].rearrange("p (r e d) -> p r e d", r=2, e=2)
                            P.op("act", lambda h, i=i, pv=pv, pr0=pr0: h.copy(vaug[:, i, pr0:pr0 + 2, 0:64], pv[:, :, 0, :]),
                                 reads=[PK[bkv]], writes=["vaug"])
                            P.op("dve", lambda h, i=i, pv=pv, pr0=pr0: h.tensor_copy(vaug[:, i, pr0:pr0 + 2, 128:192], pv[:, :, 1, :]),
                                 reads=[PK[bkv]], writes=["vaug"])
                for i in range(4):
                    dump("qT%d" % i, qT[i][:], ("q", i))
                    dump("kT%d" % i, kT[i][:], ("k", i))
                    dump("za%d" % i, za[i][:], ("za", i))
                    dump("zs%d" % i, zs[i][:], ("zs", i))
                    dump("up%d" % i, up[i][:].rearrange("p s j -> p (s j)"), ("up", i))
                dump("v0", vaug[:, 0:8, 0, :], "vaug", cols=8 * 192, view=("p (a b) -> p a b", dict(b=192)))
                P.barrier()
                if dbg:
                    dbgb[0] = P.new_batch(ordered=True)

            if upto == "p1":
                P.stopped = True
            with ExitStack() as p2:
                kmb = sbt(p2, "kmb", [128, 4, 8], BF16)
                bTall = sbt(p2, "bTall", [64, L], BF16)
                ohb = sbt(p2, "ohb", [8, L], BF16)
                pT = [sbt(p2, "pT%d" % i, [128, 512], BF16) for i in range(2)]
                den = sbt(p2, "den", [128, 512])
                dhi = sbt(p2, "dhi", [128, 512], BF16)
                dlo = sbt(p2, "dlo", [128, 512], BF16)
                onesb = sbt(p2, "onesb", [128, 128], BF16)
                rden = sbt(p2, "rden", [128, 512])
                of = sbt(p2, "of", [128, 512])
                g2 = p2.enter_context(ExitStack())
                ksum = sbt(g2, "ksum", [128, 4, 8])
                gm = sbt(g2, "gm", [128, 512])
                cm = sbt(g2, "cm", [128, 64, 8, 8], BF16)
                cnt = sbt(g2, "cnt", [128, 512])
                btok = sbt(g2, "btok", [128, 8, 64], BF16)
                ohf = sbt(g2, "ohf", [8, L])
                P.op("dve", lambda h: h.memset(onesb[:], 1.0), writes=["onesb"])
                P.op("dve", lambda h: h.memset(bTall[:, 0:1024], 0.0), writes=["bTall"])

                boh = P.new_batch()
                P.dma("sp", boh, ohf[:], oh_d, writes=["ohf"])
                P.op("dve", lambda h: h.tensor_copy(ohb[:], ohf[:]), reads=["ohf"], writes=["ohb"])
                for hp in range(4):
                    P.op("dve", lambda h, hp=hp: h.tensor_reduce(ksum[:, hp, :], kT[hp][:].rearrange("p (b k) -> p b k", k=256), AX.X, ALU.add),
                         reads=[("k", hp)], writes=["ksum"])
                P.op("dve", lambda h: h.tensor_copy(kmb[:], ksum[:]), reads=["ksum"], writes=["kmb"])
                for qt in range(8, 16):
                    for hh in range(8):
                        hp, base = hh // 2, 64 * (hh % 2)
                        col = (qt - 8) * 64 + hh * 8
                        last = (qt == 15 and hh == 7)
                        P.op("pe", lambda h, hp=hp, base=base, col=col, qt=qt: h.matmul(
                            ps[5][:, col:col + 8], qT[hp][base:base + 64, qt * 128:(qt + 1) * 128], kmb[base:base + 64, hp, :],
                            start=True, stop=True),
                            reads=[("q", hp), "kmb"], writes=[PK[5]], signal=last)
                gvv = cst[:, C_GV:C_GV + 64].rearrange("p (a b) -> p a b", b=8).unsqueeze(2).to_broadcast([128, 8, 8, 8])
                gmv = cst[:, C_GM:C_GM + 64].rearrange("p (a b) -> p a b", b=8).unsqueeze(2).to_broadcast([128, 8, 8, 8])
                gm4 = gm[:].rearrange("p (a h b) -> p a h b", h=8, b=8)
                P.op("dve", lambda h: h.tensor_tensor(gm4, ps[5][:].rearrange("p (a h b) -> p a h b", h=8, b=8), gvv, ALU.mult),
                     reads=[PK[5], "cst"], writes=["gm"])
                P.op("dve", lambda h: h.tensor_tensor(gm4, gm4, gmv, ALU.add), reads=["gm", "cst"], writes=["gm"])
                gm3 = gm[:].rearrange("p (a b) -> p a b", b=8)
                P.op("dve", lambda h: h.tensor_tensor(cm[:], gm3.unsqueeze(2).to_broadcast([128, 64, 8, 8]),
                                                      gm3.unsqueeze(3).to_broadcast([128, 64, 8, 8]), ALU.is_gt),
                     reads=["gm"], writes=["cm"])
                P.op("dve", lambda h: h.tensor_reduce(cnt[:].rearrange("p (a b) -> p a b", b=8), cm[:], AX.X, ALU.add),
                     reads=["cm"], writes=["cnt"])
                P.op("dve", lambda h: h.tensor_scalar(btok[:].rearrange("p a b -> p (a b)"), cnt[:], 3.5, NEG, ALU.is_ge, ALU.mult),
                     reads=["cnt"], writes=["btok"])
                for qi in range(8):
                    bkk = 6 + qi // 4
                    P.op("pe", lambda h, qi=qi, bkk=bkk: h.matmul(ps[bkk][0:64, (qi % 4) * 128:(qi % 4 + 1) * 128], btok[:, qi, :], identb[:],
                                                                  start=True, stop=True),
                         reads=["btok", "identb"], writes=[PK[bkk]])
                P.op("act", lambda h: h.copy(bTall[:, 1024:1536], ps[6][0:64, :]), reads=[PK[6]], writes=["bTall"])
                P.op("act", lambda h: h.copy(bTall[:, 1536:2048], ps[7][0:64, :]), reads=[PK[7]], writes=["bTall"])
                g2.close()
                P.barrier()
                if dbg:
                    dbgb[0] = P.new_batch(ordered=True)
                qa = [sbt(p2, "qa%d" % i, [128, L], BF16) for i in range(8)]
                ka = [sbt(p2, "ka%d" % i, [128, L], BF16) for i in range(8)]
                bb = P.new_batch()
                for hh in range(8):
                    hp, base = hh // 2, 64 * (hh % 2)
                    oth = 64 - base
                    e1, e2 = ("act", "dve") if hh % 2 == 0 else ("dve", "act")
                    if e1 == "act":
                        P.op("act", lambda h, hh=hh, hp=hp, base=base: h.copy(qa[hh][base:base + 64, :], qT[hp][base:base + 64, :]), reads=[("q", hp)], writes=[("qa", hh)])
                    else:
                        P.op("dve", lambda h, hh=hh, hp=hp, base=base: h.tensor_copy(qa[hh][base:base + 64, :], qT[hp][base:base + 64, :]), reads=[("q", hp)], writes=[("qa", hh)])
                    P.op("dve", lambda h, hh=hh, oth=oth: h.memset(qa[hh][oth:oth + 64, :], 0.0), writes=[("qa", hh)])
                    if e2 == "act":
                        P.op("act", lambda h, hh=hh, hp=hp: h.copy(ka[hh][:], kT[hp][:]), reads=[("k", hp)], writes=[("ka", hh)])
                    else:
                        P.op("dve", lambda h, hh=hh, hp=hp: h.tensor_copy(ka[hh][:], kT[hp][:]), reads=[("k", hp)], writes=[("ka", hh)])
                    P.dma("sp", bb, qa[hh][oth:oth + 8, :], bTall[8 * hh:8 * hh + 8, :], reads=["bTall"], writes=[("qa", hh)])
                    P.dma("sp", bb, ka[hh][oth:oth + 8, :], ohb[:], reads=["ohb"], writes=[("ka", hh)])
                dump("bTall", bTall[:], "bTall", rows=64, cols=L)
                if upto == "p2a":
                    P.stopped = True

                scale = 1.0 / 8.0
                trineg = sbt(p2, "trineg", [128, 128], BF16)
                P.op("dve", lambda h: h.tensor_scalar(trineg[:], cst[:, C_TRI:C_TRI + 128], 1.0, -NEG, ALU.subtract, ALU.mult),
                     reads=["cst"], writes=["trineg"])
                pT3 = pT + [sbt(p2, "pT2", [128, 512], BF16), sbt(p2, "pT3", [128, 512], BF16)]
                SB_ = [0, 1, 5, 6]
                pairs = [(hh, c, kt) for hh in range(8) for c in range(4) for kt in range(4 * c + 4)]
                LOOK = 2

                def emit_S(n):
                    hh, c, kt = pairs[n]
                    hp, base = hh // 2, 64 * (hh % 2)
                    qlo = max(512 * c, 128 * kt)
                    c0 = qlo - 512 * c
                    sbk = SB_[n % 4]
                    nb = (c >= 2)
                    dg = (128 * kt >= 512 * c)
                    P.op("pe", lambda h: h.matmul(
                        ps[sbk][:, c0:512], ka[hh][:, kt * 128:(kt + 1) * 128], qa[hh][:, qlo:512 * c + 512],
                        start=True, stop=not dg),
                        reads=[("ka", hh), ("qa", hh)], writes=[PK[sbk]], signal=not dg)
                    if dg:
                        P.op("pe", lambda h: h.matmul(ps[sbk][:, c0:c0 + 128], identb[:], trineg[:], start=False, stop=True),
                             reads=["identb", "trineg"], writes=[PK[sbk]])
                    P.op("act", lambda h: h.activation(pT3[n % 4][:, c0:512], ps[sbk][:, c0:512], AF.Exp, scale=scale),
                         reads=[PK[sbk]], writes=[("pT", n % 4)])

                def emit_PV(n):
                    hh, c, kt = pairs[n]
                    hp = hh // 2
                    nkt = 4 * c + 4
                    qlo = max(512 * c, 128 * kt)
                    c0 = qlo - 512 * c
                    ob = 2 + ((hh * 4 + c) % 2)
                    if hh % 2 == 0:
                        lhs = vaug[:, kt, hp, 0:65]
                        oap = ps[ob][0:65, c0:512]
                    else:
                        lhs = vaug[:, kt, hp, 64:192]
                        oap = ps[ob][:, c0:512]
                    P.op("pe", lambda h: h.matmul(oap, lhs, pT3[n % 4][:, c0:512], start=(kt == 0), stop=(kt == nkt - 1)),
                         reads=["vaug", ("pT", n % 4)], writes=[PK[ob]], signal=True)
                    if kt != nkt - 1:
                        return
                    tsl = slice(512 * c, 512 * c + 512)
                    if hh % 2 == 0:
                        dr, orr, M = slice(64, 65), slice(0, 64), 64
                    else:
                        dr, orr, M = slice(0, 1), slice(64, 128), 128
                    P.op("act", lambda h: h.copy(den[dr, :], ps[ob][dr, :]), reads=[PK[ob]], writes=["den"])
                    P.op("act", lambda h: h.copy(dhi[dr, :], den[dr, :]), reads=["den"], writes=["dhi"])
                    P.op("dve", lambda h: h.tensor_tensor(dlo[dr, :], den[dr, :], dhi[dr, :], ALU.subtract), reads=["den", "dhi"], writes=["dlo"])

                    def finish():
                        P.op("pe", lambda h: h.matmul(ps[4][0:M, :], onesb[dr, 0:M], dhi[dr, :], start=True, stop=False),
                             reads=["dhi", "onesb"], writes=[PK[4]], signal=False)
                        P.op("pe", lambda h: h.matmul(ps[4][0:M, :], onesb[dr, 0:M], dlo[dr, :], start=False, stop=True),
                             reads=["dlo", "onesb"], writes=[PK[4]])
                        P.op("dve", lambda h: h.reciprocal(rden[orr, :], ps[4][orr, :]), reads=[PK[4]], writes=["rden"])
                        P.op("dve", lambda h: h.tensor_tensor(of[orr, :], ps[ob][orr, :], rden[orr, :], ALU.mult),
                             reads=[PK[ob], "rden"], writes=["of"])
                        P.op("dve", lambda h: h.tensor_tensor(za[hp][orr, tsl], za[hp][orr, tsl], of[orr, :], ALU.mult),
                             reads=["of", ("za", hp)], writes=[("za", hp)])
                    pending.append([1, finish])

                pending = []

                def tick():
                    for it in list(pending):
                        if it[0] <= 0:
                            pending.remove(it)
                            it[1]()
                        else:
                            it[0] -= 1

                G = 2
                for n0 in range(0, len(pairs) + LOOK, G):
                    for n in range(n0, n0 + G):
                        if n < len(pairs):
                            emit_S(n)
                    tick()
                    for n in range(n0, n0 + G):
                        if LOOK <= n < len(pairs) + LOOK:
                            emit_PV(n - LOOK)
                while pending:
                    tick()
                for i in range(4):
                    dump("oa%d" % i, za[i][:], ("za", i))
                P.barrier()
                if dbg:
                    dbgb[0] = P.new_batch(ordered=True)
            qkv.close()

            if upto == "p2":
                P.stopped = True
            with ExitStack() as p3:
                yg = [sbt(p3, "yg%d" % i, [128, L], BF16) for i in range(4)]
                g1 = sbt(p3, "g1", [128, 512])
                g2 = sbt(p3, "g2", [128, 512])
                p3a = p3.enter_context(ExitStack())
                paA = sbt(p3a, "paA", [128, 4, NPA])
                pb = sbt(p3a, "pb", [128, NPB])
                W1 = [sbt(p3a, "W1%d" % i, [128, 4, 8, 128], BF16) for i in range(2)]
                W2 = [sbt(p3a, "W2%d" % i, [128, 9, 4, 128], BF16) for i in range(2)]
                Bbd = [sbt(p3a, "Bbd%d" % i, [128, 4, 128], BF16) for i in range(2)]
                Kt = sbt(p3a, "Kt", [128, 8, 128], BF16)
                Tc = sbt(p3a, "Tc", [128, 16, 256])
                Ts = sbt(p3a, "Ts", [128, 16, 256])
                tm = [W2[i][:].rearrange("p a b c -> p (a b c)").bitcast(F32)[:, 0:2048].rearrange("p (a b) -> p a b", b=128) for i in range(2)]
                Xre = sbt(p3a, "Xre", [128, 4, 256], BF16)
                Xim = sbt(p3a, "Xim", [128, 4, 256], BF16)
                vv = [sbt(p3a, "vv%d" % i, [128, 256]) for i in range(6)]
                vvb = [sbt(p3a, "vvb%d" % i, [128, 256]) for i in range(6)]
                SA = {n: sbt(p3a, "A_" + n, [128, 16]) for n in ("dt", "a", "th", "r", "rho", "c", "s", "c2", "s2", "Lre", "Lim", "e1c", "e1s",
                                                                "den", "nre", "nim", "qre", "qim", "x1", "x2")}
                SB = {n: sbt(p3a, "B_" + n, [128, 64]) for n in ("dt", "a", "th", "r", "c", "s", "c2", "s2", "Lre", "Lim",
                                                                 "den", "nre", "nim", "qre", "qim", "x1", "x2", "bbre", "bbim")}
                PAre = sbt(p3a, "PAre", [128, 9, 16])
                PAim = sbt(p3a, "PAim", [128, 9, 16])
                PBre = sbt(p3a, "PBre", [128, 8, 64])
                PBim = sbt(p3a, "PBim", [128, 8, 64])
                BAre = sbt(p3a, "BAre", [128, 4, 16])
                BAim = sbt(p3a, "BAim", [128, 4, 16])
                WVre = sbt(p3a, "WVre", [128, 9, 4, 16])
                WVim = sbt(p3a, "WVim", [128, 9, 4, 16])
                wx1 = sbt(p3a, "wx1", [128, 9, 4, 16])
                wx2 = sbt(p3a, "wx2", [128, 9, 4, 16])
                VBre = sbt(p3a, "VBre", [128, 8, 64])
                VBim = sbt(p3a, "VBim", [128, 8, 64])
                vx1 = sbt(p3a, "vx1", [128, 8, 64])
                vx2 = sbt(p3a, "vx2", [128, 8, 64])

                REC = [None]

                def V(eng, fn, r, w):
                    if REC[0] is not None:
                        REC[0].append((eng, fn, r, w))
                    else:
                        P.op(eng, fn, reads=r, writes=w)

                def record(block):
                    REC[0] = []
                    block()
                    out, REC[0] = REC[0], None
                    return out

                def run_merged(chains):
                    idx = [0] * len(chains)
                    left = sum(len(c) for c in chains)
                    while left:
                        for ci, ch in enumerate(chains):
                            if idx[ci] < len(ch):
                                eng, fn, r, w = ch[idx[ci]]
                                P.op(eng, fn, reads=r, writes=w)
                                idx[ci] += 1
                                left -= 1

                def tt(eng, o, a, b, op, r, w):
                    V(eng, lambda h: h.tensor_tensor(o, a, b, op), r, w)

                def cmul(eng, ore, oim, are, aim, bre, bim, x1, x2, r, w, xk):
                    tt(eng, x1, are, bre, ALU.mult, r, [xk + "1"])
                    tt(eng, x2, aim, bim, ALU.mult, r, [xk + "2"])
                    tt(eng, ore, x1, x2, ALU.subtract, [xk + "1", xk + "2"], w)
                    tt(eng, x1, are, bim, ALU.mult, r + w, [xk + "1"])
                    tt(eng, x2, aim, bre, ALU.mult, r + w, [xk + "2"])
                    tt(eng, oim, x1, x2, ALU.add, [xk + "1", xk + "2"], w)

                def lam_common(S, k, lre, lim, ldt, src):
                    e = "dve"
                    V("act", lambda h: h.activation(S["dt"][:, 0:1], ldt, AF.Exp), [src], [k + "dt"])
                    V(e, lambda h: h.tensor_scalar(S["a"][:], lre, S["dt"][:, 0:1], None, ALU.mult), [src, k + "dt"], [k + "a"])
                    V(e, lambda h: h.tensor_scalar(S["th"][:], lim, S["dt"][:, 0:1], None, ALU.mult), [src, k + "dt"], [k + "th"])
                    V("act", lambda h: h.activation(S["r"][:], S["a"][:], AF.Exp), [k + "a"], [k + "r"])
                    V("act", lambda h: h.activation(S["s"][:], S["th"][:], AF.Sin, scale=1.0 / 16), [k + "th"], [k + "s"])
                    V("act", lambda h: h.activation(S["c"][:], S["th"][:], AF.Sin, scale=-1.0 / 16, bias=halfpi[:]), [k + "th", "halfpi"], [k + "c"])

                    def square(n):
                        for _ in range(n):
                            tt(e, S["c2"][:], S["c"][:], S["c"][:], ALU.mult, [k + "c"], [k + "c2"])
                            tt(e, S["s2"][:], S["s"][:], S["s"][:], ALU.mult, [k + "s"], [k + "s2"])
                            V(e, lambda h: h.scalar_tensor_tensor(S["s"][:], S["c"][:], 2.0, S["s"][:], ALU.mult, ALU.mult), [k + "c", k + "s"], [k + "s"])
                            tt(e, S["c"][:], S["c2"][:], S["s2"][:], ALU.subtract, [k + "c2", k + "s2"], [k + "c"])
                    square(4)
                    tt(e, S["Lre"][:], S["r"][:], S["c"][:], ALU.mult, [k + "r", k + "c"], [k + "L"])
                    tt(e, S["Lim"][:], S["r"][:], S["s"][:], ALU.mult, [k + "r", k + "s"], [k + "L"])
                    tt(e, S["x1"][:], lre, lre, ALU.mult, [src], [k + "x1"])
                    tt(e, S["x2"][:], lim, lim, ALU.mult, [src], [k + "x2"])
                    tt(e, S["den"][:], S["x1"][:], S["x2"][:], ALU.add, [k + "x1", k + "x2"], [k + "den"])
                    V(e, lambda h: h.reciprocal(S["den"][:], S["den"][:]), [k + "den"], [k + "den"])
                    V(e, lambda h: h.scalar_tensor_tensor(S["x1"][:], S["Lre"][:], -1.0, lre, ALU.add, ALU.mult), [k + "L", src], [k + "x1"])
                    tt(e, S["x2"][:], S["Lim"][:], lim, ALU.mult, [k + "L", src], [k + "x2"])
                    tt(e, S["nre"][:], S["x1"][:], S["x2"][:], ALU.add, [k + "x1", k + "x2"], [k + "n"])
                    tt(e, S["x1"][:], S["Lim"][:], lre, ALU.mult, [k + "L", src], [k + "x1"])
                    V(e, lambda h: h.scalar_tensor_tensor(S["x2"][:], S["Lre"][:], -1.0, lim, ALU.add, ALU.mult), [k + "L", src], [k + "x2"])
                    tt(e, S["nim"][:], S["x1"][:], S["x2"][:], ALU.subtract, [k + "x1", k + "x2"], [k + "n"])
                    tt(e, S["qre"][:], S["nre"][:], S["den"][:], ALU.mult, [k + "n", k + "den"], [k + "q"])
                    tt(e, S["qim"][:], S["nim"][:], S["den"][:], ALU.mult, [k + "n", k + "den"], [k + "q"])
                    return square

                bpa = P.new_batch()
                P.dma("sp", bpa, paA[:], pa_d.rearrange("c p n -> p c n"), writes=["pa"])
                S3 = {n: SA[n][:].rearrange("p (c w) -> p c w", w=4) for n in SA}
                lre3, lim3, ldt3 = paA[:, :, 0:4], paA[:, :, 4:8], paA[:, :, 8:9]

                def sqA(nrep):
                    for _ in range(nrep):
                        tt("dve", SA["c2"][:], SA["c"][:], SA["c"][:], ALU.mult, ["Ac"], ["Ac2"])
                        tt("dve", SA["s2"][:], SA["s"][:], SA["s"][:], ALU.mult, ["As"], ["As2"])
                        V("dve", lambda h: h.scalar_tensor_tensor(SA["s"][:], SA["c"][:], 2.0, SA["s"][:], ALU.mult, ALU.mult), ["Ac", "As"], ["As"])
                        tt("dve", SA["c"][:], SA["c2"][:], SA["s2"][:], ALU.subtract, ["Ac2", "As2"], ["Ac"])

                def blkA1():
                    V("act", lambda h: h.activation(S3["dt"][:, :, 0:1], ldt3, AF.Exp), ["pa"], ["Adt"])
                    dtb = S3["dt"][:, :, 0:1].to_broadcast([128, 4, 4])
                    tt("dve", S3["a"], lre3, dtb, ALU.mult, ["pa", "Adt"], ["Aa"])
                    tt("dve", S3["th"], lim3, dtb, ALU.mult, ["pa", "Adt"], ["Ath"])
                    V("act", lambda h: h.activation(SA["r"][:], SA["a"][:], AF.Exp), ["Aa"], ["Ar"])
                    V("act", lambda h: h.activation(SA["rho"][:], SA["a"][:], AF.Exp, scale=8.0), ["Aa"], ["Arho"])
                    V("act", lambda h: h.activation(SA["s"][:], SA["th"][:], AF.Sin, scale=1.0 / 16), ["Ath"], ["As"])
                    V("act", lambda h: h.activation(SA["c"][:], SA["th"][:], AF.Sin, scale=-1.0 / 16, bias=halfpi[:]), ["Ath", "halfpi"], ["Ac"])
                    sqA(4)
                    tt("dve", SA["Lre"][:], SA["r"][:], SA["c"][:], ALU.mult, ["Ar", "Ac"], ["AL"])
                    tt("dve", SA["Lim"][:], SA["r"][:], SA["s"][:], ALU.mult, ["Ar", "As"], ["AL"])
                    tt("dve", S3["x1"], lre3, lre3, ALU.mult, ["pa"], ["Ax1"])
                    tt("dve", S3["x2"], lim3, lim3, ALU.mult, ["pa"], ["Ax2"])
                    tt("dve", SA["den"][:], SA["x1"][:], SA["x2"][:], ALU.add, ["Ax1", "Ax2"], ["Aden"])
                    V("dve", lambda h: h.reciprocal(SA["den"][:], SA["den"][:]), ["Aden"], ["Aden"])
                    V("dve", lambda h: h.scalar_tensor_tensor(S3["x1"], S3["Lre"], -1.0, lre3, ALU.add, ALU.mult), ["AL", "pa"], ["Ax1"])
                    tt("dve", S3["x2"], S3["Lim"], lim3, ALU.mult, ["AL", "pa"], ["Ax2"])
                    tt("dve", SA["nre"][:], SA["x1"][:], SA["x2"][:], ALU.add, ["Ax1", "Ax2"], ["An"])
                    tt("dve", S3["x1"], S3["Lim"], lre3, ALU.mult, ["AL", "pa"], ["Ax1"])
                    V("dve", lambda h: h.scalar_tensor_tensor(S3["x2"], S3["Lre"], -1.0, lim3, ALU.add, ALU.mult), ["AL", "pa"], ["Ax2"])
                    tt("dve", SA["nim"][:], SA["x1"][:], SA["x2"][:], ALU.subtract, ["Ax1", "Ax2"], ["An"])
                    tt("dve", SA["qre"][:], SA["nre"][:], SA["den"][:], ALU.mult, ["An", "Aden"], ["Aq"])
                    tt("dve", SA["qim"][:], SA["nim"][:], SA["den"][:], ALU.mult, ["An", "Aden"], ["Aq"])
                    sqA(3)
                    V("dve", lambda h: h.memset(Tc[:, :, 0:1], 1.0), [], ["T"])
                    V("dve", lambda h: h.memset(Ts[:, :, 0:1], 0.0), [], ["T"])
                    V("dve", lambda h: h.tensor_copy(Tc[:, :, 1], SA["c"][:]), ["Ac"], ["T"])
                    V("dve", lambda h: h.tensor_copy(Ts[:, :, 1], SA["s"][:]), ["As"], ["T"])

                def blkPA():
                    V("dve", lambda h: h.memset(PAre[:, 0, :], 1.0), [], ["PA"])
                    V("dve", lambda h: h.memset(PAim[:, 0, :], 0.0), [], ["PA"])
                    for kk in range(1, 9):
                        cmul("dve", PAre[:, kk, :], PAim[:, kk, :], PAre[:, kk - 1, :], PAim[:, kk - 1, :], SA["Lre"][:], SA["Lim"][:],
                             SA["x1"][:], SA["x2"][:], ["PA", "AL"], ["PA"], "Ax")

                def blkT():
                    for lv in range(1, 8):
                        n = 1 << lv
                        hcol = n // 2
                        tt("dve", SA["c2"][:], Tc[:, :, hcol], Tc[:, :, hcol], ALU.mult, ["T"], ["Ac2"])
                        tt("dve", SA["s2"][:], Ts[:, :, hcol], Ts[:, :, hcol], ALU.mult, ["T"], ["As2"])
                        tt("dve", Tc[:, :, n], SA["c2"][:], SA["s2"][:], ALU.subtract, ["Ac2", "As2"], ["T"])
                        V("dve", lambda h, n=n, hcol=hcol: h.scalar_tensor_tensor(Ts[:, :, n], Tc[:, :, hcol], 2.0, Ts[:, :, hcol], ALU.mult, ALU.mult),
                          ["T"], ["T"])
                        m = n - 1
                        bc_ = Tc[:, :, n:n + 1].to_broadcast([128, 16, m])
                        bs_ = Ts[:, :, n:n + 1].to_broadcast([128, 16, m])
                        cmul("dve", Tc[:, :, n + 1:2 * n], Ts[:, :, n + 1:2 * n], Tc[:, :, 1:n], Ts[:, :, 1:n], bc_, bs_,
                             tm[0][:, :, 0:m], tm[1][:, :, 0:m], ["T"], ["T"], "tm")

                vvs = [vv, vvb]
                dcolv = sbt(p3a, "dcolv", [128, 4])

                def load_pb(ct):
                    bp = P.new_batch()
                    P.dma("sp", bp, pb[:], pb_d[ct], writes=["pb"])

                def chains(ct):
                    pa = paA[:, ct, :]
                    SAv = {n_: SA[n_][:, 4 * ct:4 * ct + 4] for n_ in SA}
                    PAre_v, PAim_v = PAre[:, :, 4 * ct:4 * ct + 4], PAim[:, :, 4 * ct:4 * ct + 4]

                    def blkB():
                        lam_common(SB, "B", pb[:, 0:64], pb[:, 64:128], pb[:, 128:129], "pb")
                        cmul("dve", SB["bbre"][:], SB["bbim"][:], pb[:, 129:193], pb[:, 193:257], SB["qre"][:], SB["qim"][:],
                             SB["x1"][:], SB["x2"][:], ["pb", "Bq"], ["BB"], "Bx")
                        V("dve", lambda h: h.memset(PBre[:, 7, :], 1.0), [], ["PB"])
                        V("dve", lambda h: h.memset(PBim[:, 7, :], 0.0), [], ["PB"])
                        for sp_ in range(6, -1, -1):
                            cmul("dve", PBre[:, sp_, :], PBim[:, sp_, :], PBre[:, sp_ + 1, :], PBim[:, sp_ + 1, :], SB["Lre"][:], SB["Lim"][:],
                                 SB["x1"][:], SB["x2"][:], ["PB", "BL"], ["PB"], "Bx")
                        bbre8 = SB["bbre"][:].unsqueeze(1).to_broadcast([128, 8, 64])
                        bbim8 = SB["bbim"][:].unsqueeze(1).to_broadcast([128, 8, 64])
                        cmul("dve", VBre[:], VBim[:], PBre[:], PBim[:], bbre8, bbim8, vx1[:], vx2[:], ["PB", "BB"], ["VB"], "vx")
                        V("dve", lambda h: h.tensor_copy(dcolv[:, ct:ct + 1], pb[:, 257:258]), ["pb"], [("dcol", ct)])

                    def blkA():
                        qre_b = SAv["qre"].unsqueeze(2).to_broadcast([128, 4, 16])
                        qim_b = SAv["qim"].unsqueeze(2).to_broadcast([128, 4, 16])
                        bre_a = pa[:, 137:201].rearrange("p (q h) -> p q h", h=16)
                        bim_a = pa[:, 201:265].rearrange("p (q h) -> p q h", h=16)
                        cmul("dve", BAre[:], BAim[:], bre_a, bim_a, qre_b, qim_b, wx1[:, 0, :, :], wx2[:, 0, :, :], ["pa", "Aq"], ["BA"], "wx")
                        cre4 = pa[:, 9:73].rearrange("p (q h) -> p q h", h=16).unsqueeze(1).to_broadcast([128, 9, 4, 16])
                        cim4 = pa[:, 73:137].rearrange("p (q h) -> p q h", h=16).unsqueeze(1).to_broadcast([128, 9, 4, 16])
                        pre4 = PAre_v.unsqueeze(3).to_broadcast([128, 9, 4, 16])
                        pim4 = PAim_v.unsqueeze(3).to_broadcast([128, 9, 4, 16])
                        tt("dve", wx1[:], cre4, pre4, ALU.mult, ["pa", "PA", "BA"], ["wx1"])
                        tt("dve", wx2[:], cim4, pim4, ALU.mult, ["pa", "PA", "BA"], ["wx2"])
                        tt("dve", WVre[:], wx1[:], wx2[:], ALU.subtract, ["wx1", "wx2"], ["WV"])
                        tt("dve", wx1[:], cre4, pim4, ALU.mult, ["pa", "PA", "WV"], ["wx1"])
                        tt("dve", wx2[:], cim4, pre4, ALU.mult, ["pa", "PA", "WV"], ["wx2"])
                        V("dve", lambda h: h.scalar_tensor_tensor(WVim[:], wx1[:], -1.0, wx2[:], ALU.mult, ALU.subtract), ["wx1", "wx2"], ["WV"])
                    return record(blkB), record(blkA)

                def exp_W1():
                    for cc, VB in enumerate((VBre, VBim)):
                        vbv = VB[:].rearrange("p s (q l) -> p q s l", l=16)
                        for g8 in range(8):
                            V("act", lambda h, cc=cc, vbv=vbv, g8=g8: h.activation(
                                W1[cc][:, :, :, g8 * 16:(g8 + 1) * 16], vbv, AF.Copy, scale=bmask[:, g8:g8 + 1]),
                              ["VB", "cst"], [("W1", cc)])

                def exp_Bbd():
                    for cc, BA in enumerate((BAre, BAim)):
                        for g8 in range(8):
                            V("act", lambda h, cc=cc, BA=BA, g8=g8: h.activation(Bbd[cc][:, :, g8 * 16:(g8 + 1) * 16], BA[:], AF.Copy, scale=bmask[:, g8:g8 + 1]),
                              ["BA", "cst"], [("Bbd", cc)])

                def exp_W2():
                    for cc, WV in enumerate((WVre, WVim)):
                        for g8 in range(8):
                            V("act", lambda h, cc=cc, WV=WV, g8=g8: h.activation(
                                W2[cc][:, :, :, g8 * 16:(g8 + 1) * 16], WV[:], AF.Copy, scale=bmask[:, g8:g8 + 1]),
                              ["WV", "cst"], [("W2", cc)])

                def toeplitz(ct):
                    for l in range(8):
                        bkk = 2 + l // 4
                        n = 0
                        for cc in range(2):
                            for pq in range(4):
                                P.op("pe", lambda h, cc=cc, pq=pq, l=l, bkk=bkk, n=n: h.matmul(
                                    ps[bkk][:, (l % 4) * 128:(l % 4 + 1) * 128], Bbd[cc][:, pq, :], W2[cc][:, l, pq, :],
                                    start=(n == 0), stop=(n == 7)),
                                    reads=[("Bbd", cc), ("W2", cc)], writes=[PK[bkk]], signal=(n == 7))
                                n += 1
                    V("dve", lambda h: h.scalar_tensor_tensor(Kt[:, 0, :], ident, dcolv[:, ct:ct + 1], ps[2][:, 0:128], ALU.mult, ALU.add),
                      [PK[2], ("dcol", ct), "cst"], ["Kt"])
                    V("act", lambda h: h.copy(Kt[:, 1:4, :], ps[2][:, 128:512].rearrange("p (l m) -> p l m", m=128)), [PK[2]], ["Kt"])
                    V("act", lambda h: h.copy(Kt[:, 4:8, :], ps[3][:].rearrange("p (l m) -> p l m", m=128)), [PK[3]], ["Kt"])
                    if ct == 0:
                        dump("Kt", Kt[:].rearrange("p a b -> p (a b)"), "Kt", cols=1024)
                        dump("Tc", Tc[:, 0:4, :].rearrange("p a b -> p (a b)"), "T", cols=1024)

                def dx_matmuls(ct, pq):
                    bkk = pq % 2
                    for cc in range(2):
                        for sp_ in range(8):
                            P.op("pe", lambda h, cc=cc, sp_=sp_: h.matmul(
                                ps[bkk][:, cc * 256:(cc + 1) * 256], W1[cc][:, pq, sp_, :], up[ct][:, sp_, :],
                                start=(sp_ == 0), stop=(sp_ == 7)),
                                reads=[("W1", cc), ("up", ct)], writes=[PK[bkk]], signal=(sp_ == 7))

                def slab(ct, pq):
                    def f():
                        bkk = pq % 2
                        v_ = vvs[pq % 2]
                        vk = ["vv%d_%d" % (pq % 2, i) for i in range(6)]
                        dre, dim_ = ps[bkk][:, 0:256], ps[bkk][:, 256:512]
                        tcq, tsq = Tc[:, 4 * ct + pq, :], Ts[:, 4 * ct + pq, :]
                        tt("dve", v_[0][:], dre, tcq, ALU.mult, [PK[bkk], "T"], [vk[0]])
                        tt("dve", v_[1][:], dim_, tsq, ALU.mult, [PK[bkk], "T"], [vk[1]])
                        tt("dve", v_[2][:], v_[0][:], v_[1][:], ALU.add, [vk[0], vk[1]], [vk[2]])
                        tt("dve", v_[3][:], dim_, tcq, ALU.mult, [PK[bkk], "T"], [vk[3]])
                        tt("dve", v_[4][:], dre, tsq, ALU.mult, [PK[bkk], "T"], [vk[4]])
                        tt("dve", v_[5][:], v_[3][:], v_[4][:], ALU.subtract, [vk[3], vk[4]], [vk[5]])
                        rb = SA["rho"][:, 4 * ct + pq:4 * ct + pq + 1].to_broadcast([128, 256])
                        V("dve", lambda h: h.tensor_tensor_scan(v_[0][:], rb, v_[2][:], 0.0, ALU.mult, ALU.add), ["Arho", vk[2]], [vk[0]])
                        V("dve", lambda h: h.tensor_tensor_scan(v_[1][:], rb, v_[5][:], 0.0, ALU.mult, ALU.add), ["Arho", vk[5]], [vk[1]])
                        tt("dve", v_[2][:], v_[0][:], tcq, ALU.mult, [vk[0], "T"], [vk[2]])
                        tt("dve", v_[3][:], v_[1][:], tsq, ALU.mult, [vk[1], "T"], [vk[3]])
                        tt("dve", Xre[:, pq, :], v_[2][:], v_[3][:], ALU.subtract, [vk[2], vk[3]], [("Xre", pq)])
                        tt("dve", v_[4][:], v_[1][:], tcq, ALU.mult, [vk[1], "T"], [vk[4]])
                        tt("dve", v_[5][:], v_[0][:], tsq, ALU.mult, [vk[0], "T"], [vk[5]])
                        tt("dve", Xim[:, pq, :], v_[4][:], v_[5][:], ALU.add, [vk[4], vk[5]], [("Xim", pq)])
                    return f

                def outputs(ct):
                    for tp in range(8):
                        bkk = 4 + tp // 2
                        osl = slice((tp % 2) * 256, (tp % 2) * 256 + 256)
                        osl1 = slice((tp % 2) * 256 + 1, (tp % 2) * 256 + 256)
                        mms = [(Kt[:, tp - s_, :], up[ct][:, s_, :], False) for s_ in range(tp, -1, -1)]
                        for pq in range(4):
                            mms.append((W2[0][:, tp + 1, pq, :], Xre[:, pq, 0:255], True))
                            mms.append((W2[1][:, tp + 1, pq, :], Xim[:, pq, 0:255], True))
                        for n, (lh, rh, sh) in enumerate(mms):
                            P.op("pe", lambda h, lh=lh, rh=rh, sh=sh, n=n, bkk=bkk, osl=osl, osl1=osl1, nm=len(mms): h.matmul(
                                ps[bkk][:, osl1 if sh else osl], lh, rh, start=(n == 0), stop=(n == nm - 1)),
                                reads=["Kt", ("up", ct), ("W2", 0), ("W2", 1)] + [("Xre", q_) for q_ in range(4)] + [("Xim", q_) for q_ in range(4)], writes=[PK[bkk]], signal=(n == len(mms) - 1))
                        if tp % 2 == 1:
                            yb = ps[bkk][:]
                            if ("y%d" % ct) in dbg_d:
                                P.op("act", lambda h, yb=yb, tp=tp: h.copy(dump_sb[:, (tp // 2) * 512:(tp // 2 + 1) * 512], yb), reads=[PK[bkk]], writes=["dump_sb"])
                                if tp == 7:
                                    P.dma("sp", dbgb[0], dbg_d["y%d" % ct], dump_sb[:], reads=["dump_sb"], writes=["DBGOUT"])
                            V("act", lambda h, yb=yb: h.activation(g1[:], yb, AF.Square), [PK[bkk]], ["g1"])
                            V("dve", lambda h: h.tensor_scalar(g1[:], g1[:], 0.044715, 1.0, ALU.mult, ALU.add), ["g1"], ["g1"])
                            tt("dve", g2[:], yb, g1[:], ALU.mult, [PK[bkk], "g1"], ["g2"])
                            V("act", lambda h: h.activation(g2[:], g2[:], AF.Sigmoid, scale=1.5957691216057308), ["g2"], ["g2"])
                            ygv = yg[ct][:].rearrange("p (j t) -> p t j", t=8)[:, tp - 1:tp + 1, :]
                            tt("dve", ygv, yb.rearrange("p (t j) -> p t j", t=2), g2[:].rearrange("p (t j) -> p t j", t=2), ALU.mult,
                               [PK[bkk], "g2"], [("yg", ct)])
                    dump("yg%d" % ct, yg[ct][:], ("yg", ct))

                HEAD = 8
                load_pb(0)
                cB, cA = chains(0)
                h0 = len(cB) // 3
                run_merged([record(blkA1), cB[:h0]])
                run_merged([record(blkPA), record(blkT), cB[h0:]])
                exp_W1()
                P.barrier()
                if dbg:
                    dbgb[0] = P.new_batch(ordered=True)
                run_merged([cA])
                exp_Bbd()
                exp_W2()
                load_pb(1)
                nxt = chains(1)
                run_merged([nxt[0][:HEAD]])
                for ct in range(4):
                    if ct + 1 < 4:
                        cB, cA = nxt[0][HEAD:], nxt[1]
                    else:
                        cB, cA = [], []
                    hb = len(cB) // 2
                    pre = min(40, hb)
                    dx_matmuls(ct, 0)
                    dx_matmuls(ct, 1)
                    run_merged([cB[:pre]])
                    run_merged([record(slab(ct, 0)), record(slab(ct, 1)), cB[pre:hb], cA])
                    toeplitz(ct)
                    if ct + 1 < 4:
                        exp_Bbd()
                    dx_matmuls(ct, 2)
                    dx_matmuls(ct, 3)
                    run_merged([record(slab(ct, 2)), record(slab(ct, 3)), cB[hb:]])
                    if ct == 0:
                        dump("Xre", Xre[:].rearrange("p a b -> p (a b)"), ("Xre", 3), cols=1024)
                    outputs(ct)
                    if ct + 2 < 4:
                        load_pb(ct + 2)
                        nxt = chains(ct + 2)
                        run_merged([nxt[0][:HEAD]])
                    if ct + 1 < 4:
                        exp_W1()
                        exp_W2()
                p3a.close()
                P.barrier()
                if dbg:
                    dbgb[0] = P.new_batch(ordered=True)
                wos = [sbt(p3, "wos%d" % i, [128, 8, 512]) for i in range(2)]
                wo = sbt(p3, "wo", [128, 8, D], BF16)
                fg = sbt(p3, "fg", [128, D])
                xs4 = [sbt(p3, "xs4_%d" % i, [128, D]) for i in range(2)]
                yb_ = [sbt(p3, "yb%d" % i, [128, D]) for i in range(2)]
                ob_ = [sbt(p3, "ob%d" % i, [128, D]) for i in range(2)]
                junk4 = sbt(p3, "junk4", [128, D], BF16)
                ss2 = sbt(p3, "ss2", [128, 16])
                r2 = sbt(p3, "r2", [128, 16])
                bf_ = P.new_batch()
                P.dma("sp", bf_, fg[:], fg_d.partition_broadcast(128), writes=["fg"])
                P.op("dve", lambda h: h.memset(ss2[:], 0.0), writes=[("ss2", i_) for i_ in range(16)])
                for n in range(2):
                    bwn = P.new_batch()
                    P.dma("pool", bwn, wos[n][:], wout_d[:, n * 512:(n + 1) * 512].rearrange("(kt p) c -> p kt c", p=128), writes=[("wos", n)])
                wgs = sbt(p3, "wgs", [128, 4, 512])
                wgb = sbt(p3, "wgb", [128, 4, 512], BF16)
                bg = P.new_batch()
                P.dma("sp", bg, wgs[:], wglu_d.rearrange("(ct p) e -> p ct e", p=128), writes=["wgs"])
                P.op("act", lambda h: h.copy(wgb[:], wgs[:]), reads=["wgs"], writes=["wgb"])
                for et in range(4):
                    if et in (1, 2):
                        P.op("act", lambda h, n=et - 1: h.copy(wo[:, :, n * 512:(n + 1) * 512], wos[n][:]), reads=[("wos", n_ := et - 1)], writes=[("wo", et - 1)])
                    for tch in range(4):
                        tsl = slice(tch * 512, (tch + 1) * 512)
                        bkk = (et * 4 + tch) % 4
                        for ct in range(4):
                            P.op("pe", lambda h, et=et, ct=ct, tsl=tsl, bkk=bkk: h.matmul(
                                ps[bkk][:], wgb[:, ct, et * 128:(et + 1) * 128], yg[ct][:, tsl], start=(ct == 0), stop=(ct == 3)),
                                reads=["wgb", ("yg", ct)], writes=[PK[bkk]], signal=(ct == 3))
                        gk = "g1" if tch % 2 == 0 else "g2"
                        gt_ = g1 if tch % 2 == 0 else g2
                        V("act", lambda h, et=et, bkk=bkk, gt_=gt_: h.activation(gt_[:], ps[bkk][:], AF.Sigmoid, bias=bglu[:, et:et + 1]),
                          [PK[bkk], "cst"], [gk])
                        tt("dve", gt_[:], gt_[:], yg[et][:, tsl], ALU.mult, [gk, ("yg", et)], [gk])
                        tt("dve", zs[et][:, tsl], zs[et][:, tsl], gt_[:], ALU.mult, [gk, ("zs", et)], [("zs", et)])
                for i in range(4):
                    dump("os%d" % i, zs[i][:], ("zs", i))
                if upto == "p3":
                    P.stopped = True
                mix = za + zs
                bxs = [P.new_batch() for _ in range(16)]
                bout = [P.new_batch(ordered=True) for _ in range(2)]
                def stage_a(i):
                    s2 = i % 2
                    P.dma("pool", bxs[i], xs4[s2][:], x_d[i * 128:(i + 1) * 128, :], writes=[("xs", s2)])
                    for n in range(2):
                        bkk = 2 * s2 + n
                        for kt in range(8):
                            P.op("pe", lambda h, n=n, kt=kt, bkk=bkk: h.matmul(
                                ps[bkk][:], mix[kt][:, i * 128:(i + 1) * 128], wo[:, kt, n * 512:(n + 1) * 512], start=(kt == 0), stop=(kt == 7)),
                                reads=[("wo", n), (("za", kt) if kt < 4 else ("zs", kt - 4))], writes=[PK[bkk]], signal=(kt == 7))
                        P.op("dve", lambda h, n=n, bkk=bkk: h.tensor_tensor(yb_[s2][:, n * 512:(n + 1) * 512], ps[bkk][:], xs4[s2][:, n * 512:(n + 1) * 512], ALU.add),
                             reads=[PK[bkk], ("xs", s2)], writes=[("yb", s2)])
                    P.op("act", lambda h: h.activation(junk4[:], yb_[s2][:], AF.Square, accum_out=ss2[:, i:i + 1]),
                         reads=[("yb", s2)], writes=["junk", ("ss2", i)])
                    P.op("dve", lambda h: h.tensor_scalar(r2[:, i:i + 1], ss2[:, i:i + 1], 1.0 / D, EPS, ALU.mult, ALU.add),
                         reads=[("ss2", i)], writes=[("r2", i)])
                    P.op("act", lambda h: h.activation(r2[:, i:i + 1], r2[:, i:i + 1], AF.Sqrt), reads=[("r2", i)], writes=[("r2", i)])

                def stage_b(i):
                    s2 = i % 2
                    P.op("dve", lambda h: h.reciprocal(r2[:, i:i + 1], r2[:, i:i + 1]), reads=[("r2", i)], writes=[("r2", i)])
                    P.op("dve", lambda h: h.scalar_tensor_tensor(ob_[s2][:], yb_[s2][:], r2[:, i:i + 1], fg[:], ALU.mult, ALU.mult),
                         reads=[("yb", s2), ("r2", i), "fg"], writes=[("ob", s2)])
                    P.dma("sp", bout[s2], out_d[i * 128:(i + 1) * 128, :], ob_[s2][:], reads=[("ob", s2)], writes=["OUT"])

                for i in range(17):
                    if i < 16:
                        stage_a(i)
                    if i >= 1:
                        stage_b(i - 1)
                P.barrier()
                if dbg:
                    dbgb[0] = P.new_batch(ordered=True)
        except _Stop:
            qkv.close()
            P.barrier()
        P.emit()
    return nc


def _host_consts():
    cst = np.zeros((128, NCST), np.float32)
    cst[:, C_ID:C_ID + 128] = np.eye(128, dtype=np.float32)
    k = np.arange(128)
    cst[:, C_TRI:C_TRI + 128] = (k[None, :] >= k[:, None]).astype(np.float32)
    cst[:, C_BM:C_BM + 8] = (k[:, None] // 16 == np.arange(8)[None, :]).astype(np.float32)
    gv = np.zeros((8, 8), np.float32)
    gm = np.zeros((8, 8), np.float32)
    for a in range(8):
        own = (8 + a) // 2
        for b in range(8):
            if b < own:
                gv[a, b] = 1.0
            elif b == own:
                gm[a, b] = 1e9
            else:
                gm[a, b] = -1e9
    cst[:, C_GV:C_GV + 64] = gv.reshape(1, 64)
    cst[:, C_GM:C_GM + 64] = gm.reshape(1, 64)
    pm = np.zeros((128, 128), np.float32)
    for m in range(128):
        pm[64 * (m // 64) + ((m % 64) + 32) % 64, m] = 1.0
    cst[:, C_PERM:C_PERM + 128] = pm
    half = 32
    inv = (1.0 / (10000.0 ** (np.arange(half, dtype=np.float32) / half))).astype(np.float32)
    ang = (np.arange(L, dtype=np.float32)[:, None] * inv[None, :]).astype(np.float32).astype(np.float64)
    cos, sin = np.cos(ang).T, np.sin(ang).T
    c64 = np.concatenate([cos, cos], 0)
    s64 = np.concatenate([-sin, sin], 0)
    rope = np.stack([np.concatenate([c64, c64], 0), np.concatenate([s64, s64], 0)], 1).astype(np.float32)
    onehot = (np.arange(L)[None, :] // 256 == np.arange(8)[:, None]).astype(np.float32)
    return cst, rope, onehot


def _prep_shared(norm_gain, w_in, w_out, lam_re, lam_im, b_re, b_im, c_re, c_im, d_skip, log_dt, w_glu, b_glu, final_gain):
    cst, rope, onehot = _host_consts()
    cst[:, C_G:C_G + 8] = norm_gain[0].reshape(8, 128).T
    cst[:, C_BG:C_BG + 4] = b_glu[0].reshape(4, 128).T
    W = w_in[0]
    A = 512
    q, k_, v, zatt, u, zss = W[:, 0:A], W[:, A:2 * A], W[:, 2 * A:3 * A], W[:, 3 * A:4 * A], W[:, 4 * A:5 * A], W[:, 5 * A:6 * A]
    chunks = [q[:, 0:256], q[:, 256:512], k_[:, 0:256], k_[:, 256:512]]
    for m in (zatt, u, zss, v):
        chunks.append(m[:, 0:256])
        chunks.append(m[:, 256:512])
    w_in_c = np.ascontiguousarray(np.stack(chunks, 0)).astype(np.float32)
    pA = np.zeros((4, 128, NPA), np.float32)
    pB = np.zeros((4, 128, NPB), np.float32)
    lre, lim, ldt = lam_re[0], lam_im[0], log_dt[0]
    bre, bim, cre, cim, dsk = b_re[0], b_im[0], c_re[0], c_im[0], d_skip[0]
    for ct in range(4):
        g = slice(8 * ct, 8 * ct + 8)

        def A_(m):
            return m.reshape(8, 4, 16).transpose(0, 2, 1).reshape(128, 4)
        pA[ct, :, 0:4] = A_(lre[g])
        pA[ct, :, 4:8] = A_(lim[g])
        pA[ct, :, 8] = np.repeat(ldt[g], 16)
        pA[ct, :, 9:73] = cre[g].reshape(8, 16, 4, 16).transpose(0, 3, 2, 1).reshape(128, 64)
        pA[ct, :, 73:137] = cim[g].reshape(8, 16, 4, 16).transpose(0, 3, 2, 1).reshape(128, 64)
        pA[ct, :, 137:201] = bre[g].reshape(8, 4, 16, 16).transpose(0, 2, 1, 3).reshape(128, 64)
        pA[ct, :, 201:265] = bim[g].reshape(8, 4, 16, 16).transpose(0, 2, 1, 3).reshape(128, 64)
        pB[ct, :, 0:64] = np.repeat(lre[g], 16, axis=0)
        pB[ct, :, 64:128] = np.repeat(lim[g], 16, axis=0)
        pB[ct, :, 128] = np.repeat(ldt[g], 16)
        pB[ct, :, 129:193] = bre[g].transpose(0, 2, 1).reshape(128, 64)
        pB[ct, :, 193:257] = bim[g].transpose(0, 2, 1).reshape(128, 64)
        pB[ct, :, 257] = dsk[g].reshape(128)
    return dict(w_in_c=w_in_c, w_out=np.ascontiguousarray(w_out[0]), w_glu=np.ascontiguousarray(w_glu[0]), cst=cst, rope=rope,
                fgain=np.ascontiguousarray(final_gain.reshape(1, D)), onehot=onehot, pA=pA, pB=pB)


def kernel(x, norm_gain, w_in, w_out, lam_re, lam_im, b_re, b_im, c_re, c_im, d_skip, log_dt, w_glu, b_glu, final_gain):
    args = [np.asarray(a, dtype=np.float32) for a in (norm_gain, w_in, w_out, lam_re, lam_im, b_re, b_im, c_re, c_im,
                                                      d_skip, log_dt, w_glu, b_glu, final_gain)]
    x = np.asarray(x, dtype=np.float32)
    shared = _prep_shared(*args)
    nc = build()
    in_maps = []
    for b in range(8):
        m = dict(shared)
        m["x"] = np.ascontiguousarray(x[b])
        in_maps.append(m)
    res = run_bass_kernel_spmd(nc, in_maps, core_ids=list(range(8)))
    return np.stack([np.asarray(r["out"], dtype=np.float32) for r in res.results], 0)
```

```python
import math
from contextlib import ExitStack
import numpy as np
import concourse.bass as bass
import concourse.mybir as mybir
from concourse.bass_utils import run_bass_kernel_spmd

F32 = mybir.dt.float32
BF16 = mybir.dt.bfloat16
ALU = mybir.AluOpType
AF = mybir.ActivationFunctionType
AX = mybir.AxisListType

L = 2048
D = 1024
NEG = -30000.0
EPS = 1e-6


SAME_ENGINE_IN_ORDER = ("pe",)


class _Eng:
    def __init__(self, name, sem):
        self.name = name
        self.sem = sem
        self.count = 0
        self.ops = []
        self.waited = {}


class _Batch:
    def __init__(self, sem, ordered=False):
        self.sem = sem
        self.n = 0
        self.ordered = ordered


class Prog:
    def __init__(self, nc, ctx):
        self.nc = nc
        self.ctx = ctx
        self.engs = {}
        for name in ("pe", "act", "dve", "pool", "sp"):
            sem = ctx.enter_context(nc.semaphore("s_" + name))
            self.engs[name] = _Eng(name, sem)
        self.res = {}
        self.batches = []
        self.stopped = False

    def _st(self, key):
        st = self.res.get(key)
        if st is None:
            st = {"w": None, "r": []}
            self.res[key] = st
        return st

    def _add_waits(self, eng, deps):
        waits = []
        e = self.engs[eng]
        for d in deps:
            if d[0] == "e":
                _, en, idx = d
                if en == eng and eng in SAME_ENGINE_IN_ORDER:
                    continue
                if e.waited.get(en, 0) >= idx:
                    continue
                e.waited[en] = idx
                waits = [w for w in waits if not (w[0] == "e" and w[1] == en)]
                waits.append(d)
            else:
                b = d[1]
                k = d[2] if b.ordered else 1 << 30
                if e.waited.get(id(b), 0) >= k:
                    continue
                e.waited[id(b)] = k
                waits = [w for w in waits if not (w[0] == "b" and w[1] is b)]
                waits.append(d)
        return waits

    def _deps(self, eng, reads, writes):
        deps = []
        for k in reads:
            st = self._st(k)
            if st["w"] is not None:
                deps.append(st["w"])
        for k in writes:
            st = self._st(k)
            if st["w"] is not None:
                deps.append(st["w"])
            deps.extend(st["r"])
        return self._add_waits(eng, deps)

    def _update(self, ev, reads, writes):
        for k in reads:
            self._st(k)["r"].append(ev)
        for k in writes:
            st = self._st(k)
            st["w"] = ev
            st["r"] = []

    def op(self, eng, fn, reads=(), writes=(), signal=True):
        if self.stopped:
            return
        e = self.engs[eng]
        waits = self._deps(eng, reads, writes)
        ev = ("e", eng, e.count + 1)
        if signal:
            e.count += 1
        e.ops.append((waits, fn, (e.sem, 1) if signal else None))
        self._update(ev, reads, writes)

    def new_batch(self, ordered=False):
        sem = self.ctx.enter_context(self.nc.semaphore("s_dma%d" % len(self.batches)))
        b = _Batch(sem, ordered)
        self.batches.append(b)
        return b

    def dma(self, eng, batch, out, in_, reads=(), writes=()):
        if self.stopped:
            return
        e = self.engs[eng]
        waits = self._deps(eng, reads, writes)
        batch.n += 1
        ev = ("b", batch, batch.n)

        def fn(h, out=out, in_=in_):
            return h.dma_start(out=out, in_=in_)
        e.ops.append((waits, fn, (batch.sem, 16)))
        self._update(ev, reads, writes)

    def barrier(self):
        for name, e in self.engs.items():
            deps = [("e", en, o.count) for en, o in self.engs.items() if en != name and o.count > 0]
            deps += [("b", b, b.n) for b in self.batches if b.n > 0]
            waits = self._add_waits(name, deps)
            e.ops.append((waits, None, None))
        self.res = {}

    def emit(self):
        nc = self.nc
        with nc.Block() as block:
            def replay(name, h):
                e = self.engs[name]
                for waits, fn, inc in e.ops:
                    for w in waits:
                        if w[0] == "e":
                            h.wait_ge(self.engs[w[1]].sem, w[2])
                        else:
                            h.wait_ge(w[1].sem, 16 * (w[2] if w[1].ordered else w[1].n))
                    if fn is None:
                        continue
                    ins = fn(h)
                    if inc is not None:
                        ins.then_inc(inc[0], inc[1])

            @block.tensor
            def _(h):
                replay("pe", h)

            @block.scalar
            def _(h):
                replay("act", h)

            @block.vector
            def _(h):
                replay("dve", h)

            @block.gpsimd
            def _(h):
                replay("pool", h)

            @block.sync
            def _(h):
                replay("sp", h)


C_ID = 0
C_TRI = 128
C_G = 256
C_BG = 264
C_BM = 268
C_GV = 276
C_GM = 340
C_PERM = 404
NCST = 532
NPA = 265
NPB = 258


class _Stop(Exception):
    pass


def build(dbg=None, upto=None):
    dbg = dbg or {}
    nc = bass.Bass("TRN2", target_bir_lowering=False)

    def din(name, shape):
        return nc.dram_tensor(name, shape, F32, kind="ExternalInput").ap()
    x_d = din("x", [L, D])
    win_d = din("w_in_c", [12, D, 256])
    wout_d = din("w_out", [D, D])
    wglu_d = din("w_glu", [512, 512])
    cst_d = din("cst", [128, NCST])
    rope_d = din("rope", [128, 2, L])
    fg_d = din("fgain", [1, D])
    oh_d = din("onehot", [8, L])
    pa_d = din("pA", [4, 128, NPA])
    pb_d = din("pB", [4, 128, NPB])
    out_d = nc.dram_tensor("out", [L, D], F32, kind="ExternalOutput").ap()
    dbg_d = {k: nc.dram_tensor("dbg_" + k, [128, L], F32, kind="ExternalOutput").ap() for k, s in dbg.items()}

    with ExitStack() as ctx:
        P = Prog(nc, ctx)

        def sbt(c, name, shape, dt=F32):
            return c.enter_context(nc.sbuf_tensor("sb_" + name, shape, dt))

        ps = [ctx.enter_context(nc.psum_tensor("ps%d" % i, [128, 512], F32)) for i in range(8)]
        PK = ["ps%d" % i for i in range(8)]

        cst = sbt(ctx, "cst", [128, NCST])
        identb = sbt(ctx, "identb", [128, 128], BF16)
        trib = sbt(ctx, "trib", [128, 128], BF16)
        onesf = sbt(ctx, "onesf", [128, 128])
        halfpi = sbt(ctx, "halfpi", [128, 1])
        za = [sbt(ctx, "za%d" % i, [128, L], BF16) for i in range(4)]
        zs = [sbt(ctx, "zs%d" % i, [128, L], BF16) for i in range(4)]
        up = [sbt(ctx, "up%d" % i, [128, 8, 256], BF16) for i in range(4)]
        dump_sb = sbt(ctx, "dump_sb", [128, L]) if dbg else None

        ident = cst[:, C_ID:C_ID + 128]
        gcol = cst[:, C_G:C_G + 8]
        bglu = cst[:, C_BG:C_BG + 4]
        bmask = cst[:, C_BM:C_BM + 8]

        b0 = P.new_batch()
        P.dma("sp", b0, cst[:], cst_d, writes=["cst"])
        P.op("dve", lambda h: h.tensor_copy(identb[:], ident), reads=["cst"], writes=["identb"])
        P.op("dve", lambda h: h.tensor_copy(trib[:], cst[:, C_TRI:C_TRI + 128]), reads=["cst"], writes=["trib"])
        P.op("dve", lambda h: h.memset(onesf[:], 1.0), writes=["onesf"])
        P.op("dve", lambda h: h.memset(halfpi[:], math.pi / 2), writes=["halfpi"])

        dbgb = [P.new_batch(ordered=True) if dbg else None]

        def dump(name, src_ap, key, rows=128, cols=L, row0=0, col0=0, view=None):
            if name not in dbg_d:
                return
            dv = dump_sb[0:rows, 0:cols]
            if view is not None:
                dv = dv.rearrange(view[0], **view[1])
            P.op("dve", lambda h: h.tensor_copy(dv, src_ap), reads=[key], writes=["dump_sb"])
            P.dma("sp", dbgb[0], dbg_d[name], dump_sb[:], reads=["dump_sb"], writes=["DBGOUT"])

        try:
            qkv = ctx.enter_context(ExitStack())
            qT = [sbt(qkv, "qT%d" % i, [128, L], BF16) for i in range(4)]
            kT = [sbt(qkv, "kT%d" % i, [128, L], BF16) for i in range(4)]
            vaug = sbt(qkv, "vaug", [128, 16, 4, 192], BF16)
            P.op("dve", lambda h: h.memset(vaug[:], 0.0), writes=["vaug"])
            P.op("dve", lambda h: h.memset(vaug[:, :, :, 64:65], 1.0), writes=["vaug"])

            with ExitStack() as p1:
                xs = [sbt(p1, "xs%d" % i, [128, D]) for i in range(3)]
                xbf = [sbt(p1, "xbf%d" % i, [128, D], BF16) for i in range(3)]
                junk = sbt(p1, "junk", [128, D], BF16)
                hT = sbt(p1, "hT", [128, 8, L], BF16)
                wst = [sbt(p1, "wst%d" % i, [128, 8, 256]) for i in range(2)]
                wbf = [sbt(p1, "wbf%d" % i, [128, 8, 256], BF16) for i in range(2)]
                ropef = sbt(p1, "ropef", [128, 512])
                rope = sbt(p1, "rope", [128, 2, L], BF16)
                ss = sbt(p1, "ss", [128, 16])
                rst = sbt(p1, "rst", [128, 16])
                Dt = [sbt(p1, "Dt%d" % i, [128, 128]) for i in range(2)]
                rbc = [sbt(p1, "rbc%d" % i, [128, 128]) for i in range(2)]
                t1 = sbt(p1, "t1", [128, 512])
                t2 = sbt(p1, "t2", [128, 512])
                t3 = sbt(p1, "t3", [128, 512])
                permb = sbt(p1, "permb", [128, 128], BF16)
                qbf = [sbt(p1, "qbf%d" % i, [128, 512], BF16) for i in range(2)]
                P.op("dve", lambda h: h.tensor_copy(permb[:], cst[:, C_PERM:C_PERM + 128]), reads=["cst"], writes=["permb"])

                P.op("dve", lambda h: h.memset(ss[:], 0.0), writes=[("ss", i_) for i_ in range(16)])
                brope = P.new_batch()
                bw = [P.new_batch() for _ in range(12)]
                bx = [P.new_batch() for _ in range(16)]

                ropef2 = [ropef, sbt(p1, "ropef_b", [128, 512])]

                def load_rope():
                    n = 0
                    for tch in range(4):
                        for cs in range(2):
                            br = P.new_batch()
                            stg = ropef2[n % 2]
                            P.dma("sp", br, stg[:], rope_d[:, cs, tch * 512:(tch + 1) * 512], writes=[("ropef", n % 2)])
                            P.op("dve", lambda h, cs=cs, tch=tch, stg=stg: h.tensor_copy(rope[:, cs, tch * 512:(tch + 1) * 512], stg[:]),
                                 reads=[("ropef", n % 2)], writes=["rope"])
                            n += 1

                def load_w(c):
                    P.dma("sp", bw[c], wst[c % 2][:], win_d[c].rearrange("(kt p) c -> p kt c", p=128),
                          writes=[("wst", c % 2)])

                load_w(0)
                load_w(1)
                def st_a(i):
                    s2, s1 = i % 3, i % 2
                    P.dma("sp", bx[i], xs[s2][:], x_d[i * 128:(i + 1) * 128, :], writes=[("xs", s2)])
                    P.op("act", lambda h: h.activation(junk[:], xs[s2][:], AF.Square, accum_out=ss[:, i:i + 1]),
                         reads=[("xs", s2)], writes=["junk", ("ss", i)])
                    P.op("act", lambda h: h.copy(xbf[s2][:], xs[s2][:]), reads=[("xs", s2)], writes=[("xbf", s2)])
                    trv = ps[s1][:].bitcast(BF16)
                    for kt in range(8):
                        P.op("pe", lambda h, kt=kt: h.transpose(trv[:, kt * 128:(kt + 1) * 128], xbf[s2][:, kt * 128:(kt + 1) * 128], identb[:]),
                             reads=[("xbf", s2), "identb"], writes=[PK[s1]], signal=(kt == 7))

                def st_b(i):
                    s1 = i % 2
                    trv = ps[s1][:].bitcast(BF16)
                    P.op("dve", lambda h: h.tensor_scalar(rst[:, i:i + 1], ss[:, i:i + 1], 1.0 / D, EPS, ALU.mult, ALU.add),
                         reads=[("ss", i)], writes=[("rst", i)])
                    P.op("act", lambda h: h.activation(rst[:, i:i + 1], rst[:, i:i + 1], AF.Sqrt), reads=[("rst", i)], writes=[("rst", i)])
                    P.op("dve", lambda h: h.reciprocal(rst[:, i:i + 1], rst[:, i:i + 1]), reads=[("rst", i)], writes=[("rst", i)])
                    P.op("dve", lambda h: h.tensor_scalar(Dt[s1][:], ident, rst[:, i:i + 1], None, ALU.mult),
                         reads=[("rst", i), "cst"], writes=[("Dt", s1)])
                    P.op("pe", lambda h: h.matmul(ps[2 + s1][:, 0:128], onesf[:], Dt[s1][:], start=True, stop=True),
                         reads=[("Dt", s1), "onesf"], writes=[PK[2 + s1]])
                    P.op("act", lambda h: h.copy(rbc[s1][:], ps[2 + s1][:, 0:128]), reads=[PK[2 + s1]], writes=[("rbc", s1)])
                    P.op("dve", lambda h: h.tensor_tensor(
                        hT[:, :, i * 128:(i + 1) * 128], trv.rearrange("p (k t) -> p k t", k=8),
                        rbc[s1][:].unsqueeze(1).to_broadcast([128, 8, 128]), ALU.mult),
                        reads=[PK[s1], ("rbc", s1)], writes=["hT"])

                for i in range(17):
                    if i < 16:
                        st_a(i)
                    if i >= 1:
                        st_b(i - 1)
                if upto == "x16":
                    P.stopped = True
                if upto == "xr":
                    P.stopped = True
                dump("xs", xs[1][:], ("xs", 1), cols=1024)
                dump("ss", ss[:], ("ss", 15), cols=16)
                dump("rst", rst[:], ("rst", 15), cols=16)
                dump("rbc", rbc[1][:], ("rbc", 1), cols=128)
                dump("xbf", xbf[1][:], ("xbf", 1), cols=1024)
                dump("Dt", Dt[1][:], ("Dt", 1), cols=128)
                dump("hT0", hT[:, 0, :], "hT")
                if upto == "p1a":
                    P.stopped = True

                def cast_chunk(c):
                    sl = c % 2
                    for kt in range(8):
                        if kt % 2 == 0:
                            P.op("act", lambda h, kt=kt: h.activation(wbf[sl][:, kt, :], wst[sl][:, kt, :], AF.Copy, scale=gcol[:, kt:kt + 1]),
                                 reads=[("wst", sl), "cst"], writes=[("wbf", sl)])
                        else:
                            P.op("dve", lambda h, kt=kt: h.tensor_scalar(wbf[sl][:, kt, :], wst[sl][:, kt, :], gcol[:, kt:kt + 1], None, ALU.mult),
                                 reads=[("wst", sl), "cst"], writes=[("wbf", sl)])
                    if c == 0:
                        load_rope()
                    if c + 2 < 12:
                        load_w(c + 2)

                pend_qk = []
                for c in range(12):
                    sl = c % 2
                    if upto is not None and upto.startswith('c') and upto[1:].isdigit() and c == int(upto[1:]):
                        P.stopped = True
                    if c == 0:
                        cast_chunk(0)
                    if c < 10:
                        for tch in range(4):
                            if tch == 1 and c + 1 < 12:
                                cast_chunk(c + 1)
                            tsl = slice(tch * 512, (tch + 1) * 512)
                            bk = [4 + 2 * (tch % 2), 5 + 2 * (tch % 2)]
                            for g in range(2):
                                for kt in range(8):
                                    P.op("pe", lambda h, g=g, kt=kt, sl=sl, tsl=tsl, bk=bk: h.matmul(
                                        ps[bk[g]][:], wbf[sl][:, kt, g * 128:(g + 1) * 128], hT[:, kt, tsl],
                                        start=(kt == 0), stop=(kt == 7)),
                                        reads=[("wbf", sl), "hT"], writes=[PK[bk[g]]], signal=(kt == 7))
                            if c < 4:
                                def qk_evac(tsl=tsl, bk=bk, c=c):
                                    for g in range(2):
                                        hp_ = 2 * (c % 2) + g
                                        dst = qT[hp_] if c < 2 else kT[hp_]
                                        dk = ("q", hp_) if c < 2 else ("k", hp_)
                                        ta, tb = (t1, t2) if g == 0 else (t3, t2)
                                        tak = "t1" if g == 0 else "t3"
                                        P.op("act", lambda h, g=g, bk=bk: h.copy(qbf[g][:], ps[bk[g]][:]), reads=[PK[bk[g]]], writes=[("qbf", g)])
                                        P.op("pe", lambda h, g=g: h.matmul(ps[2 + g][:], permb[:], qbf[g][:], start=True, stop=True),
                                             reads=["permb", ("qbf", g)], writes=[PK[2 + g]])
                                        P.op("dve", lambda h, g=g, tsl=tsl, bk=bk, ta=ta: h.tensor_tensor(ta[:], ps[bk[g]][:], rope[:, 0, tsl], ALU.mult),
                                             reads=[PK[bk[g]], "rope", ("qbf", g)], writes=[tak])
                                        P.op("dve", lambda h, g=g, tsl=tsl: h.tensor_tensor(t2[:], ps[2 + g][:], rope[:, 1, tsl], ALU.mult),
                                             reads=[PK[2 + g], "rope"], writes=["t2"])
                                        P.op("dve", lambda h, dst=dst, tsl=tsl, ta=ta: h.tensor_tensor(dst[:, tsl], ta[:], t2[:], ALU.add),
                                             reads=[tak, "t2"], writes=[dk])
                                if pend_qk:
                                    pend_qk.pop()()
                                pend_qk.append(qk_evac)
                                if tch == 3:
                                    pend_qk.pop()()
                            elif c in (4, 5, 8, 9):
                                for g in range(2):
                                    dst = za[2 * (c - 4) + g] if c < 6 else zs[2 * (c - 8) + g]
                                    dk = ("za", 2 * (c - 4) + g) if c < 6 else ("zs", 2 * (c - 8) + g)
                                    tt = t1 if g == 0 else t2
                                    tk = "t1" if g == 0 else "t2"
                                    P.op("act", lambda h, g=g, bk=bk, tt=tt: h.activation(tt[:], ps[bk[g]][:], AF.Sigmoid),
                                         reads=[PK[bk[g]]], writes=[tk])
                                    P.op("dve", lambda h, g=g, bk=bk, tt=tt, dst=dst, tsl=tsl: h.tensor_tensor(dst[:, tsl], ps[bk[g]][:], tt[:], ALU.mult),
                                         reads=[PK[bk[g]], tk], writes=[dk])
                            else:
                                for g in range(2):
                                    ct = 2 * (c - 6) + g
                                    dstv = up[ct][:].rearrange("p s j -> p j s")[:, tch * 64:(tch + 1) * 64, :]
                                    P.op("act", lambda h, g=g, bk=bk, dstv=dstv: h.copy(dstv, ps[bk[g]][:].rearrange("p (j s) -> p j s", s=8)),
                                         reads=[PK[bk[g]]], writes=[("up", ct)])
                    else:
                        pr0 = 2 * (c - 10)
                        for i in range(16):
                            if i == 4 and c + 1 < 12:
                                cast_chunk(c + 1)
                            bkv = 4 + (i % 4)
                            for kt in range(8):
                                P.op("pe", lambda h, i=i, kt=kt, sl=sl, bkv=bkv: h.matmul(
                                    ps[bkv][:, 0:256], hT[:, kt, i * 128:(i + 1) * 128], wbf[sl][:, kt, :],
                                    start=(kt == 0), stop=(kt == 7)),
                                    reads=[("wbf", sl), "hT"], writes=[PK[bkv]], signal=(kt == 7))
                            pv = ps[bkv][:, 0:256].rearrange("p (r e d) -> p r e d", r=2, e=2)
                            P.op("act", lambda h, i=i, pv=pv, pr0=pr0: h.copy(vaug[:, i, pr0:pr0 + 2, 0:64], pv[:, :, 0, :]),
                                 reads=[PK[bkv]], writes=["vaug"])
                            P.op("dve", lambda h, i=i, pv=pv, pr0=pr0: h.tensor_copy(vaug[:, i, pr0:pr0 + 2, 128:192], pv[:, :, 1, :]),
                                 reads=[PK[bkv]], writes=["vaug"])
                for i in range(4):
                    dump("qT%d" % i, qT[i][:], ("q", i))
                    dump("kT%d" % i, kT[i][:], ("k", i))
                    dump("za%d" % i, za[i][:], ("za", i))
                    dump("zs%d" % i, zs[i][:], ("zs", i))
                    dump("up%d" % i, up[i][:].rearrange("p s j -> p (s j)"), ("up", i))
                dump("v0", vaug[:, 0:8, 0, :], "vaug", cols=8 * 192, view=("p (a b) -> p a b", dict(b=192)))
                P.barrier()
                if dbg:
                    dbgb[0] = P.new_batch(ordered=True)

            if upto == "p1":
                P.stopped = True
            with ExitStack() as p2:
                kmb = sbt(p2, "kmb", [128, 4, 8], BF16)
                bTall = sbt(p2, "bTall", [64, L], BF16)
                ohb = sbt(p2, "ohb", [8, L], BF16)
                pT = [sbt(p2, "pT%d" % i, [128, 512], BF16) for i in range(2)]
                den = sbt(p2, "den", [128, 512])
                dhi = sbt(p2, "dhi", [128, 512], BF16)
                dlo = sbt(p2, "dlo", [128, 512], BF16)
                onesb = sbt(p2, "onesb", [128, 128], BF16)
                rden = sbt(p2, "rden", [128, 512])
                of = sbt(p2, "of", [128, 512])
                g2 = p2.enter_context(ExitStack())
                ksum = sbt(g2, "ksum", [128, 4, 8])
                gm = sbt(g2, "gm", [128, 512])
                cm = sbt(g2, "cm", [128, 64, 8, 8], BF16)
                cnt = sbt(g2, "cnt", [128, 512])
                btok = sbt(g2, "btok", [128, 8, 64], BF16)
                ohf = sbt(g2, "ohf", [8, L])
                P.op("dve", lambda h: h.memset(onesb[:], 1.0), writes=["onesb"])
                P.op("dve", lambda h: h.memset(bTall[:, 0:1024], 0.0), writes=["bTall"])

                boh = P.new_batch()
                P.dma("sp", boh, ohf[:], oh_d, writes=["ohf"])
                for hp in range(4):
                    P.op("dve", lambda h, hp=hp: h.tensor_reduce(ksum[:, hp, :], kT[hp][:].rearrange("p (b k) -> p b k", k=256), AX.X, ALU.add),
                         reads=[("k", hp)], writes=["ksum"])
                P.op("dve", lambda h: h.tensor_copy(kmb[:], ksum[:]), reads=["ksum"], writes=["kmb"])
                for qt in range(8, 16):
                    for hh in range(8):
                        hp, base = hh // 2, 64 * (hh % 2)
                        col = (qt - 8) * 64 + hh * 8
                        last = (qt == 15 and hh == 7)
                        P.op("pe", lambda h, hp=hp, base=base, col=col, qt=qt: h.matmul(
                            ps[5][:, col:col + 8], qT[hp][base:base + 64, qt * 128:(qt + 1) * 128], kmb[base:base + 64, hp, :],
                            start=True, stop=True),
                            reads=[("q", hp), "kmb"], writes=[PK[5]], signal=last)
                gvv = cst[:, C_GV:C_GV + 64].rearrange("p (a b) -> p a b", b=8).unsqueeze(2).to_broadcast([128, 8, 8, 8])
                gmv = cst[:, C_GM:C_GM + 64].rearrange("p (a b) -> p a b", b=8).unsqueeze(2).to_broadcast([128, 8, 8, 8])
                gm4 = gm[:].rearrange("p (a h b) -> p a h b", h=8, b=8)
                P.op("dve", lambda h: h.tensor_tensor(gm4, ps[5][:].rearrange("p (a h b) -> p a h b", h=8, b=8), gvv, ALU.mult),
                     reads=[PK[5], "cst"], writes=["gm"])
                P.op("dve", lambda h: h.tensor_tensor(gm4, gm4, gmv, ALU.add), reads=["gm", "cst"], writes=["gm"])
                gm3 = gm[:].rearrange("p (a b) -> p a b", b=8)
                P.op("dve", lambda h: h.tensor_tensor(cm[:], gm3.unsqueeze(2).to_broadcast([128, 64, 8, 8]),
                                                      gm3.unsqueeze(3).to_broadcast([128, 64, 8, 8]), ALU.is_gt),
                     reads=["gm"], writes=["cm"])
                P.op("dve", lambda h: h.tensor_reduce(cnt[:].rearrange("p (a b) -> p a b", b=8), cm[:], AX.X, ALU.add),
                     reads=["cm"], writes=["cnt"])
                P.op("dve", lambda h: h.tensor_scalar(btok[:].rearrange("p a b -> p (a b)"), cnt[:], 3.5, NEG, ALU.is_ge, ALU.mult),
                     reads=["cnt"], writes=["btok"])
                for qi in range(8):
                    bkk = 6 + qi // 4
                    P.op("pe", lambda h, qi=qi, bkk=bkk: h.matmul(ps[bkk][0:64, (qi % 4) * 128:(qi % 4 + 1) * 128], btok[:, qi, :], identb[:],
                                                                  start=True, stop=True),
                         reads=["btok", "identb"], writes=[PK[bkk]])
                P.op("act", lambda h: h.copy(bTall[:, 1024:1536], ps[6][0:64, :]), reads=[PK[6]], writes=["bTall"])
                P.op("act", lambda h: h.copy(bTall[:, 1536:2048], ps[7][0:64, :]), reads=[PK[7]], writes=["bTall"])
                P.op("dve", lambda h: h.tensor_copy(ohb[:], ohf[:]), reads=["ohf"], writes=["ohb"])
                g2.close()
                P.barrier()
                if dbg:
                    dbgb[0] = P.new_batch(ordered=True)
                qa = [sbt(p2, "qa%d" % i, [128, L], BF16) for i in range(8)]
                ka = [sbt(p2, "ka%d" % i, [128, L], BF16) for i in range(8)]
                bb = P.new_batch()
                for hh in range(8):
                    hp, base = hh // 2, 64 * (hh % 2)
                    oth = 64 - base
                    e1, e2 = ("act", "dve") if hh % 2 == 0 else ("dve", "act")
                    if e1 == "act":
                        P.op("act", lambda h, hh=hh, hp=hp, base=base: h.copy(qa[hh][base:base + 64, :], qT[hp][base:base + 64, :]), reads=[("q", hp)], writes=[("qa", hh)])
                    else:
                        P.op("dve", lambda h, hh=hh, hp=hp, base=base: h.tensor_copy(qa[hh][base:base + 64, :], qT[hp][base:base + 64, :]), reads=[("q", hp)], writes=[("qa", hh)])
                    P.op("dve", lambda h, hh=hh, oth=oth: h.memset(qa[hh][oth:oth + 64, :], 0.0), writes=[("qa", hh)])
                    if e2 == "act":
                        P.op("act", lambda h, hh=hh, hp=hp: h.copy(ka[hh][:], kT[hp][:]), reads=[("k", hp)], writes=[("ka", hh)])
                    else:
                        P.op("dve", lambda h, hh=hh, hp=hp: h.tensor_copy(ka[hh][:], kT[hp][:]), reads=[("k", hp)], writes=[("ka", hh)])
                    P.dma("sp", bb, qa[hh][oth:oth + 8, :], bTall[8 * hh:8 * hh + 8, :], reads=["bTall"], writes=[("qa", hh)])
                    P.dma("sp", bb, ka[hh][oth:oth + 8, :], ohb[:], reads=["ohb"], writes=[("ka", hh)])
                dump("bTall", bTall[:], "bTall", rows=64, cols=L)
                if upto == "p2a":
                    P.stopped = True

                scale = 1.0 / 8.0
                trineg = sbt(p2, "trineg", [128, 128], BF16)
                P.op("dve", lambda h: h.tensor_scalar(trineg[:], cst[:, C_TRI:C_TRI + 128], 1.0, -NEG, ALU.subtract, ALU.mult),
                     reads=["cst"], writes=["trineg"])
                pT3 = pT + [sbt(p2, "pT2", [128, 512], BF16), sbt(p2, "pT3", [128, 512], BF16)]
                SB_ = [0, 1, 5, 6]
                pairs = [(hh, c, kt) for hh in range(8) for c in range(4) for kt in range(4 * c + 4)]
                LOOK = 2

                def emit_S(n):
                    hh, c, kt = pairs[n]
                    hp, base = hh // 2, 64 * (hh % 2)
                    qlo = max(512 * c, 128 * kt)
                    c0 = qlo - 512 * c
                    sbk = SB_[n % 4]
                    nb = (c >= 2)
                    dg = (128 * kt >= 512 * c)
                    P.op("pe", lambda h: h.matmul(
                        ps[sbk][:, c0:512], ka[hh][:, kt * 128:(kt + 1) * 128], qa[hh][:, qlo:512 * c + 512],
                        start=True, stop=not dg),
                        reads=[("ka", hh), ("qa", hh)], writes=[PK[sbk]], signal=not dg)
                    if dg:
                        P.op("pe", lambda h: h.matmul(ps[sbk][:, c0:c0 + 128], identb[:], trineg[:], start=False, stop=True),
                             reads=["identb", "trineg"], writes=[PK[sbk]])
                    P.op("act", lambda h: h.activation(pT3[n % 4][:, c0:512], ps[sbk][:, c0:512], AF.Exp, scale=scale),
                         reads=[PK[sbk]], writes=[("pT", n % 4)])

                def emit_PV(n):
                    hh, c, kt = pairs[n]
                    hp = hh // 2
                    nkt = 4 * c + 4
                    qlo = max(512 * c, 128 * kt)
                    c0 = qlo - 512 * c
                    ob = 2 + ((hh * 4 + c) % 2)
                    if hh % 2 == 0:
                        lhs = vaug[:, kt, hp, 0:65]
                        oap = ps[ob][0:65, c0:512]
                    else:
                        lhs = vaug[:, kt, hp, 64:192]
                        oap = ps[ob][:, c0:512]
                    P.op("pe", lambda h: h.matmul(oap, lhs, pT3[n % 4][:, c0:512], start=(kt == 0), stop=(kt == nkt - 1)),
                         reads=["vaug", ("pT", n % 4)], writes=[PK[ob]], signal=True)
                    if kt != nkt - 1:
                        return
                    tsl = slice(512 * c, 512 * c + 512)
                    if hh % 2 == 0:
                        dr, orr, M = slice(64, 65), slice(0, 64), 64
                    else:
                        dr, orr, M = slice(0, 1), slice(64, 128), 128
                    P.op("act", lambda h: h.copy(den[dr, :], ps[ob][dr, :]), reads=[PK[ob]], writes=["den"])
                    P.op("act", lambda h: h.copy(dhi[dr, :], den[dr, :]), reads=["den"], writes=["dhi"])
                    P.op("dve", lambda h: h.tensor_tensor(dlo[dr, :], den[dr, :], dhi[dr, :], ALU.subtract), reads=["den", "dhi"], writes=["dlo"])

                    def finish():
                        P.op("pe", lambda h: h.matmul(ps[4][0:M, :], onesb[dr, 0:M], dhi[dr, :], start=True, stop=False),
                             reads=["dhi", "onesb"], writes=[PK[4]], signal=False)
                        P.op("pe", lambda h: h.matmul(ps[4][0:M, :], onesb[dr, 0:M], dlo[dr, :], start=False, stop=True),
                             reads=["dlo", "onesb"], writes=[PK[4]])
                        P.op("dve", lambda h: h.reciprocal(rden[orr, :], ps[4][orr, :]), reads=[PK[4]], writes=["rden"])
                        P.op("dve", lambda h: h.tensor_tensor(of[orr, :], ps[ob][orr, :], rden[orr, :], ALU.mult),
                             reads=[PK[ob], "rden"], writes=["of"])
                        P.op("dve", lambda h: h.tensor_tensor(za[hp][orr, tsl], za[hp][orr, tsl], of[orr, :], ALU.mult),
                             reads=["of", ("za", hp)], writes=[("za", hp)])
                    pending.append([1, finish])

                pending = []

                def tick():
                    for it in list(pending):
                        if it[0] <= 0:
                            pending.remove(it)
                            it[1]()
                        else:
                            it[0] -= 1

                G = 2
                for n0 in range(0, len(pairs) + LOOK, G):
                    for n in range(n0, n0 + G):
                        if n < len(pairs):
                            emit_S(n)
                    tick()
                    for n in range(n0, n0 + G):
                        if LOOK <= n < len(pairs) + LOOK:
                            emit_PV(n - LOOK)
                while pending:
                    tick()
                for i in range(4):
                    dump("oa%d" % i, za[i][:], ("za", i))
                P.barrier()
                if dbg:
                    dbgb[0] = P.new_batch(ordered=True)
            qkv.close()

            if upto == "p2":
                P.stopped = True
            with ExitStack() as p3:
                yg = [sbt(p3, "yg%d" % i, [128, L], BF16) for i in range(4)]
                g1 = sbt(p3, "g1", [128, 512])
                g2 = sbt(p3, "g2", [128, 512])
                p3a = p3.enter_context(ExitStack())
                paA = sbt(p3a, "paA", [128, 4, NPA])
                pb = sbt(p3a, "pb", [128, NPB])
                W1 = [sbt(p3a, "W1%d" % i, [128, 4, 8, 128], BF16) for i in range(2)]
                W2 = [sbt(p3a, "W2%d" % i, [128, 9, 4, 128], BF16) for i in range(2)]
                Bbd = [sbt(p3a, "Bbd%d" % i, [128, 4, 128], BF16) for i in range(2)]
                Kt = sbt(p3a, "Kt", [128, 8, 128], BF16)
                Tc = sbt(p3a, "Tc", [128, 16, 256])
                Ts = sbt(p3a, "Ts", [128, 16, 256])
                tm = [W2[i][:].rearrange("p a b c -> p (a b c)").bitcast(F32)[:, 0:2048].rearrange("p (a b) -> p a b", b=128) for i in range(2)]
                Xre = sbt(p3a, "Xre", [128, 4, 256], BF16)
                Xim = sbt(p3a, "Xim", [128, 4, 256], BF16)
                vv = [sbt(p3a, "vv%d" % i, [128, 256]) for i in range(6)]
                vvb = [sbt(p3a, "vvb%d" % i, [128, 256]) for i in range(6)]
                SA = {n: sbt(p3a, "A_" + n, [128, 16]) for n in ("dt", "a", "th", "r", "rho", "c", "s", "c2", "s2", "Lre", "Lim", "e1c", "e1s",
                                                                "den", "nre", "nim", "qre", "qim", "x1", "x2")}
                SB = {n: sbt(p3a, "B_" + n, [128, 64]) for n in ("dt", "a", "th", "r", "c", "s", "c2", "s2", "Lre", "Lim",
                                                                 "den", "nre", "nim", "qre", "qim", "x1", "x2", "bbre", "bbim")}
                PAre = sbt(p3a, "PAre", [128, 9, 16])
                PAim = sbt(p3a, "PAim", [128, 9, 16])
                PBre = sbt(p3a, "PBre", [128, 8, 64])
                PBim = sbt(p3a, "PBim", [128, 8, 64])
                BAre = sbt(p3a, "BAre", [128, 4, 16])
                BAim = sbt(p3a, "BAim", [128, 4, 16])
                WVre = sbt(p3a, "WVre", [128, 9, 4, 16])
                WVim = sbt(p3a, "WVim", [128, 9, 4, 16])
                wx1 = sbt(p3a, "wx1", [128, 9, 4, 16])
                wx2 = sbt(p3a, "wx2", [128, 9, 4, 16])
                VBre = sbt(p3a, "VBre", [128, 8, 64])
                VBim = sbt(p3a, "VBim", [128, 8, 64])
                vx1 = sbt(p3a, "vx1", [128, 8, 64])
                vx2 = sbt(p3a, "vx2", [128, 8, 64])

                REC = [None]

                def V(eng, fn, r, w):
                    if REC[0] is not None:
                        REC[0].append((eng, fn, r, w))
                    else:
                        P.op(eng, fn, reads=r, writes=w)

                def record(block):
                    REC[0] = []
                    block()
                    out, REC[0] = REC[0], None
                    return out

                def run_merged(chains):
                    idx = [0] * len(chains)
                    left = sum(len(c) for c in chains)
                    while left:
                        for ci, ch in enumerate(chains):
                            if idx[ci] < len(ch):
                                eng, fn, r, w = ch[idx[ci]]
                                P.op(eng, fn, reads=r, writes=w)
                                idx[ci] += 1
                                left -= 1

                def tt(eng, o, a, b, op, r, w):
                    V(eng, lambda h: h.tensor_tensor(o, a, b, op), r, w)

                def cmul(eng, ore, oim, are, aim, bre, bim, x1, x2, r, w, xk):
                    tt(eng, x1, are, bre, ALU.mult, r, [xk + "1"])
                    tt(eng, x2, aim, bim, ALU.mult, r, [xk + "2"])
                    tt(eng, ore, x1, x2, ALU.subtract, [xk + "1", xk + "2"], w)
                    tt(eng, x1, are, bim, ALU.mult, r + w, [xk + "1"])
                    tt(eng, x2, aim, bre, ALU.mult, r + w, [xk + "2"])
                    tt(eng, oim, x1, x2, ALU.add, [xk + "1", xk + "2"], w)

                def lam_common(S, k, lre, lim, ldt, src):
                    e = "dve"
                    V("act", lambda h: h.activation(S["dt"][:, 0:1], ldt, AF.Exp), [src], [k + "dt"])
                    V(e, lambda h: h.tensor_scalar(S["a"][:], lre, S["dt"][:, 0:1], None, ALU.mult), [src, k + "dt"], [k + "a"])
                    V(e, lambda h: h.tensor_scalar(S["th"][:], lim, S["dt"][:, 0:1], None, ALU.mult), [src, k + "dt"], [k + "th"])
                    V("act", lambda h: h.activation(S["r"][:], S["a"][:], AF.Exp), [k + "a"], [k + "r"])
                    V("act", lambda h: h.activation(S["s"][:], S["th"][:], AF.Sin, scale=1.0 / 16), [k + "th"], [k + "s"])
                    V("act", lambda h: h.activation(S["c"][:], S["th"][:], AF.Sin, scale=-1.0 / 16, bias=halfpi[:]), [k + "th", "halfpi"], [k + "c"])

                    def square(n):
                        for _ in range(n):
                            tt(e, S["c2"][:], S["c"][:], S["c"][:], ALU.mult, [k + "c"], [k + "c2"])
                            tt(e, S["s2"][:], S["s"][:], S["s"][:], ALU.mult, [k + "s"], [k + "s2"])
                            V(e, lambda h: h.scalar_tensor_tensor(S["s"][:], S["c"][:], 2.0, S["s"][:], ALU.mult, ALU.mult), [k + "c", k + "s"], [k + "s"])
                            tt(e, S["c"][:], S["c2"][:], S["s2"][:], ALU.subtract, [k + "c2", k + "s2"], [k + "c"])
                    square(4)
                    tt(e, S["Lre"][:], S["r"][:], S["c"][:], ALU.mult, [k + "r", k + "c"], [k + "L"])
                    tt(e, S["Lim"][:], S["r"][:], S["s"][:], ALU.mult, [k + "r", k + "s"], [k + "L"])
                    tt(e, S["x1"][:], lre, lre, ALU.mult, [src], [k + "x1"])
                    tt(e, S["x2"][:], lim, lim, ALU.mult, [src], [k + "x2"])
                    tt(e, S["den"][:], S["x1"][:], S["x2"][:], ALU.add, [k + "x1", k + "x2"], [k + "den"])
                    V(e, lambda h: h.reciprocal(S["den"][:], S["den"][:]), [k + "den"], [k + "den"])
                    V(e, lambda h: h.scalar_tensor_tensor(S["x1"][:], S["Lre"][:], -1.0, lre, ALU.add, ALU.mult), [k + "L", src], [k + "x1"])
                    tt(e, S["x2"][:], S["Lim"][:], lim, ALU.mult, [k + "L", src], [k + "x2"])
                    tt(e, S["nre"][:], S["x1"][:], S["x2"][:], ALU.add, [k + "x1", k + "x2"], [k + "n"])
                    tt(e, S["x1"][:], S["Lim"][:], lre, ALU.mult, [k + "L", src], [k + "x1"])
                    V(e, lambda h: h.scalar_tensor_tensor(S["x2"][:], S["Lre"][:], -1.0, lim, ALU.add, ALU.mult), [k + "L", src], [k + "x2"])
                    tt(e, S["nim"][:], S["x1"][:], S["x2"][:], ALU.subtract, [k + "x1", k + "x2"], [k + "n"])
                    tt(e, S["qre"][:], S["nre"][:], S["den"][:], ALU.mult, [k + "n", k + "den"], [k + "q"])
                    tt(e, S["qim"][:], S["nim"][:], S["den"][:], ALU.mult, [k + "n", k + "den"], [k + "q"])
                    return square

                bpa = P.new_batch()
                P.dma("sp", bpa, paA[:], pa_d.rearrange("c p n -> p c n"), writes=["pa"])
                S3 = {n: SA[n][:].rearrange("p (c w) -> p c w", w=4) for n in SA}
                lre3, lim3, ldt3 = paA[:, :, 0:4], paA[:, :, 4:8], paA[:, :, 8:9]

                def sqA(nrep):
                    for _ in range(nrep):
                        tt("dve", SA["c2"][:], SA["c"][:], SA["c"][:], ALU.mult, ["Ac"], ["Ac2"])
                        tt("dve", SA["s2"][:], SA["s"][:], SA["s"][:], ALU.mult, ["As"], ["As2"])
                        V("dve", lambda h: h.scalar_tensor_tensor(SA["s"][:], SA["c"][:], 2.0, SA["s"][:], ALU.mult, ALU.mult), ["Ac", "As"], ["As"])
                        tt("dve", SA["c"][:], SA["c2"][:], SA["s2"][:], ALU.subtract, ["Ac2", "As2"], ["Ac"])

                def blkA1():
                    V("act", lambda h: h.activation(S3["dt"][:, :, 0:1], ldt3, AF.Exp), ["pa"], ["Adt"])
                    dtb = S3["dt"][:, :, 0:1].to_broadcast([128, 4, 4])
                    tt("dve", S3["a"], lre3, dtb, ALU.mult, ["pa", "Adt"], ["Aa"])
                    tt("dve", S3["th"], lim3, dtb, ALU.mult, ["pa", "Adt"], ["Ath"])
                    V("act", lambda h: h.activation(SA["r"][:], SA["a"][:], AF.Exp), ["Aa"], ["Ar"])
                    V("act", lambda h: h.activation(SA["rho"][:], SA["a"][:], AF.Exp, scale=8.0), ["Aa"], ["Arho"])
                    V("act", lambda h: h.activation(SA["s"][:], SA["th"][:], AF.Sin, scale=1.0 / 16), ["Ath"], ["As"])
                    V("act", lambda h: h.activation(SA["c"][:], SA["th"][:], AF.Sin, scale=-1.0 / 16, bias=halfpi[:]), ["Ath", "halfpi"], ["Ac"])
                    sqA(4)
                    tt("dve", SA["Lre"][:], SA["r"][:], SA["c"][:], ALU.mult, ["Ar", "Ac"], ["AL"])
                    tt("dve", SA["Lim"][:], SA["r"][:], SA["s"][:], ALU.mult, ["Ar", "As"], ["AL"])
                    tt("dve", S3["x1"], lre3, lre3, ALU.mult, ["pa"], ["Ax1"])
                    tt("dve", S3["x2"], lim3, lim3, ALU.mult, ["pa"], ["Ax2"])
                    tt("dve", SA["den"][:], SA["x1"][:], SA["x2"][:], ALU.add, ["Ax1", "Ax2"], ["Aden"])
                    V("dve", lambda h: h.reciprocal(SA["den"][:], SA["den"][:]), ["Aden"], ["Aden"])
                    V("dve", lambda h: h.scalar_tensor_tensor(S3["x1"], S3["Lre"], -1.0, lre3, ALU.add, ALU.mult), ["AL", "pa"], ["Ax1"])
                    tt("dve", S3["x2"], S3["Lim"], lim3, ALU.mult, ["AL", "pa"], ["Ax2"])
                    tt("dve", SA["nre"][:], SA["x1"][:], SA["x2"][:], ALU.add, ["Ax1", "Ax2"], ["An"])
                    tt("dve", S3["x1"], S3["Lim"], lre3, ALU.mult, ["AL", "pa"], ["Ax1"])
                    V("dve", lambda h: h.scalar_tensor_tensor(S3["x2"], S3["Lre"], -1.0, lim3, ALU.add, ALU.mult), ["AL", "pa"], ["Ax2"])
                    tt("dve", SA["nim"][:], SA["x1"][:], SA["x2"][:], ALU.subtract, ["Ax1", "Ax2"], ["An"])
                    tt("dve", SA["qre"][:], SA["nre"][:], SA["den"][:], ALU.mult, ["An", "Aden"], ["Aq"])
                    tt("dve", SA["qim"][:], SA["nim"][:], SA["den"][:], ALU.mult, ["An", "Aden"], ["Aq"])
                    sqA(3)
                    V("dve", lambda h: h.memset(Tc[:, :, 0:1], 1.0), [], ["T"])
                    V("dve", lambda h: h.memset(Ts[:, :, 0:1], 0.0), [], ["T"])
                    V("dve", lambda h: h.tensor_copy(Tc[:, :, 1], SA["c"][:]), ["Ac"], ["T"])
                    V("dve", lambda h: h.tensor_copy(Ts[:, :, 1], SA["s"][:]), ["As"], ["T"])

                def blkPA():
                    V("dve", lambda h: h.memset(PAre[:, 0, :], 1.0), [], ["PA"])
                    V("dve", lambda h: h.memset(PAim[:, 0, :], 0.0), [], ["PA"])
                    for kk in range(1, 9):
                        cmul("dve", PAre[:, kk, :], PAim[:, kk, :], PAre[:, kk - 1, :], PAim[:, kk - 1, :], SA["Lre"][:], SA["Lim"][:],
                             SA["x1"][:], SA["x2"][:], ["PA", "AL"], ["PA"], "Ax")

                def blkT():
                    for lv in range(1, 8):
                        n = 1 << lv
                        hcol = n // 2
                        tt("dve", SA["c2"][:], Tc[:, :, hcol], Tc[:, :, hcol], ALU.mult, ["T"], ["Ac2"])
                        tt("dve", SA["s2"][:], Ts[:, :, hcol], Ts[:, :, hcol], ALU.mult, ["T"], ["As2"])
                        tt("dve", Tc[:, :, n], SA["c2"][:], SA["s2"][:], ALU.subtract, ["Ac2", "As2"], ["T"])
                        V("dve", lambda h, n=n, hcol=hcol: h.scalar_tensor_tensor(Ts[:, :, n], Tc[:, :, hcol], 2.0, Ts[:, :, hcol], ALU.mult, ALU.mult),
                          ["T"], ["T"])
                        m = n - 1
                        bc_ = Tc[:, :, n:n + 1].to_broadcast([128, 16, m])
                        bs_ = Ts[:, :, n:n + 1].to_broadcast([128, 16, m])
                        cmul("dve", Tc[:, :, n + 1:2 * n], Ts[:, :, n + 1:2 * n], Tc[:, :, 1:n], Ts[:, :, 1:n], bc_, bs_,
                             tm[0][:, :, 0:m], tm[1][:, :, 0:m], ["T"], ["T"], "tm")

                vvs = [vv, vvb]
                dcolv = sbt(p3a, "dcolv", [128, 4])

                def load_pb(ct):
                    bp = P.new_batch()
                    P.dma("sp", bp, pb[:], pb_d[ct], writes=["pb"])

                def chains(ct):
                    pa = paA[:, ct, :]
                    SAv = {n_: SA[n_][:, 4 * ct:4 * ct + 4] for n_ in SA}
                    PAre_v, PAim_v = PAre[:, :, 4 * ct:4 * ct + 4], PAim[:, :, 4 * ct:4 * ct + 4]

                    def blkB():
                        lam_common(SB, "B", pb[:, 0:64], pb[:, 64:128], pb[:, 128:129], "pb")
                        cmul("dve", SB["bbre"][:], SB["bbim"][:], pb[:, 129:193], pb[:, 193:257], SB["qre"][:], SB["qim"][:],
                             SB["x1"][:], SB["x2"][:], ["pb", "Bq"], ["BB"], "Bx")
                        V("dve", lambda h: h.memset(PBre[:, 7, :], 1.0), [], ["PB"])
                        V("dve", lambda h: h.memset(PBim[:, 7, :], 0.0), [], ["PB"])
                        for sp_ in range(6, -1, -1):
                            cmul("dve", PBre[:, sp_, :], PBim[:, sp_, :], PBre[:, sp_ + 1, :], PBim[:, sp_ + 1, :], SB["Lre"][:], SB["Lim"][:],
                                 SB["x1"][:], SB["x2"][:], ["PB", "BL"], ["PB"], "Bx")
                        bbre8 = SB["bbre"][:].unsqueeze(1).to_broadcast([128, 8, 64])
                        bbim8 = SB["bbim"][:].unsqueeze(1).to_broadcast([128, 8, 64])
                        cmul("dve", VBre[:], VBim[:], PBre[:], PBim[:], bbre8, bbim8, vx1[:], vx2[:], ["PB", "BB"], ["VB"], "vx")
                        V("dve", lambda h: h.tensor_copy(dcolv[:, ct:ct + 1], pb[:, 257:258]), ["pb"], [("dcol", ct)])

                    def blkA():
                        qre_b = SAv["qre"].unsqueeze(2).to_broadcast([128, 4, 16])
                        qim_b = SAv["qim"].unsqueeze(2).to_broadcast([128, 4, 16])
                        bre_a = pa[:, 137:201].rearrange("p (q h) -> p q h", h=16)
                        bim_a = pa[:, 201:265].rearrange("p (q h) -> p q h", h=16)
                        cmul("dve", BAre[:], BAim[:], bre_a, bim_a, qre_b, qim_b, wx1[:, 0, :, :], wx2[:, 0, :, :], ["pa", "Aq"], ["BA"], "wx")
                        cre4 = pa[:, 9:73].rearrange("p (q h) -> p q h", h=16).unsqueeze(1).to_broadcast([128, 9, 4, 16])
                        cim4 = pa[:, 73:137].rearrange("p (q h) -> p q h", h=16).unsqueeze(1).to_broadcast([128, 9, 4, 16])
                        pre4 = PAre_v.unsqueeze(3).to_broadcast([128, 9, 4, 16])
                        pim4 = PAim_v.unsqueeze(3).to_broadcast([128, 9, 4, 16])
                        tt("dve", wx1[:], cre4, pre4, ALU.mult, ["pa", "PA", "BA"], ["wx1"])
                        tt("dve", wx2[:], cim4, pim4, ALU.mult, ["pa", "PA", "BA"], ["wx2"])
                        tt("dve", WVre[:], wx1[:], wx2[:], ALU.subtract, ["wx1", "wx2"], ["WV"])
                        tt("dve", wx1[:], cre4, pim4, ALU.mult, ["pa", "PA", "WV"], ["wx1"])
                        tt("dve", wx2[:], cim4, pre4, ALU.mult, ["pa", "PA", "WV"], ["wx2"])
                        V("dve", lambda h: h.scalar_tensor_tensor(WVim[:], wx1[:], -1.0, wx2[:], ALU.mult, ALU.subtract), ["wx1", "wx2"], ["WV"])
                    return record(blkB), record(blkA)

                def exp_W1():
                    for cc, VB in enumerate((VBre, VBim)):
                        vbv = VB[:].rearrange("p s (q l) -> p q s l", l=16)
                        for g8 in range(8):
                            V("act", lambda h, cc=cc, vbv=vbv, g8=g8: h.activation(
                                W1[cc][:, :, :, g8 * 16:(g8 + 1) * 16], vbv, AF.Copy, scale=bmask[:, g8:g8 + 1]),
                              ["VB", "cst"], [("W1", cc)])

                def exp_Bbd():
                    for cc, BA in enumerate((BAre, BAim)):
                        for g8 in range(8):
                            V("act", lambda h, cc=cc, BA=BA, g8=g8: h.activation(Bbd[cc][:, :, g8 * 16:(g8 + 1) * 16], BA[:], AF.Copy, scale=bmask[:, g8:g8 + 1]),
                              ["BA", "cst"], [("Bbd", cc)])

                def exp_W2():
                    for cc, WV in enumerate((WVre, WVim)):
                        for g8 in range(8):
                            V("act", lambda h, cc=cc, WV=WV, g8=g8: h.activation(
                                W2[cc][:, :, :, g8 * 16:(g8 + 1) * 16], WV[:], AF.Copy, scale=bmask[:, g8:g8 + 1]),
                              ["WV", "cst"], [("W2", cc)])

                def toeplitz(ct):
                    for l in range(8):
                        bkk = 2 + l // 4
                        n = 0
                        for cc in range(2):
                            for pq in range(4):
                                P.op("pe", lambda h, cc=cc, pq=pq, l=l, bkk=bkk, n=n: h.matmul(
                                    ps[bkk][:, (l % 4) * 128:(l % 4 + 1) * 128], Bbd[cc][:, pq, :], W2[cc][:, l, pq, :],
                                    start=(n == 0), stop=(n == 7)),
                                    reads=[("Bbd", cc), ("W2", cc)], writes=[PK[bkk]], signal=(n == 7))
                                n += 1
                    V("dve", lambda h: h.scalar_tensor_tensor(Kt[:, 0, :], ident, dcolv[:, ct:ct + 1], ps[2][:, 0:128], ALU.mult, ALU.add),
                      [PK[2], ("dcol", ct), "cst"], ["Kt"])
                    V("act", lambda h: h.copy(Kt[:, 1:4, :], ps[2][:, 128:512].rearrange("p (l m) -> p l m", m=128)), [PK[2]], ["Kt"])
                    V("act", lambda h: h.copy(Kt[:, 4:8, :], ps[3][:].rearrange("p (l m) -> p l m", m=128)), [PK[3]], ["Kt"])
                    if ct == 0:
                        dump("Kt", Kt[:].rearrange("p a b -> p (a b)"), "Kt", cols=1024)
                        dump("Tc", Tc[:, 0:4, :].rearrange("p a b -> p (a b)"), "T", cols=1024)

                def dx_matmuls(ct, pq):
                    bkk = pq % 2
                    for cc in range(2):
                        for sp_ in range(8):
                            P.op("pe", lambda h, cc=cc, sp_=sp_: h.matmul(
                                ps[bkk][:, cc * 256:(cc + 1) * 256], W1[cc][:, pq, sp_, :], up[ct][:, sp_, :],
                                start=(sp_ == 0), stop=(sp_ == 7)),
                                reads=[("W1", cc), ("up", ct)], writes=[PK[bkk]], signal=(sp_ == 7))

                def slab(ct, pq):
                    def f():
                        bkk = pq % 2
                        v_ = vvs[pq % 2]
                        vk = ["vv%d_%d" % (pq % 2, i) for i in range(6)]
                        dre, dim_ = ps[bkk][:, 0:256], ps[bkk][:, 256:512]
                        tcq, tsq = Tc[:, 4 * ct + pq, :], Ts[:, 4 * ct + pq, :]
                        tt("dve", v_[0][:], dre, tcq, ALU.mult, [PK[bkk], "T"], [vk[0]])
                        tt("dve", v_[1][:], dim_, tsq, ALU.mult, [PK[bkk], "T"], [vk[1]])
                        tt("dve", v_[2][:], v_[0][:], v_[1][:], ALU.add, [vk[0], vk[1]], [vk[2]])
                        tt("dve", v_[3][:], dim_, tcq, ALU.mult, [PK[bkk], "T"], [vk[3]])
                        tt("dve", v_[4][:], dre, tsq, ALU.mult, [PK[bkk], "T"], [vk[4]])
                        tt("dve", v_[5][:], v_[3][:], v_[4][:], ALU.subtract, [vk[3], vk[4]], [vk[5]])
                        rb = SA["rho"][:, 4 * ct + pq:4 * ct + pq + 1].to_broadcast([128, 256])
                        V("dve", lambda h: h.tensor_tensor_scan(v_[0][:], rb, v_[2][:], 0.0, ALU.mult, ALU.add), ["Arho", vk[2]], [vk[0]])
                        V("dve", lambda h: h.tensor_tensor_scan(v_[1][:], rb, v_[5][:], 0.0, ALU.mult, ALU.add), ["Arho", vk[5]], [vk[1]])
                        tt("dve", v_[2][:], v_[0][:], tcq, ALU.mult, [vk[0], "T"], [vk[2]])
                        tt("dve", v_[3][:], v_[1][:], tsq, ALU.mult, [vk[1], "T"], [vk[3]])
                        tt("dve", Xre[:, pq, :], v_[2][:], v_[3][:], ALU.subtract, [vk[2], vk[3]], [("Xre", pq)])
                        tt("dve", v_[4][:], v_[1][:], tcq, ALU.mult, [vk[1], "T"], [vk[4]])
                        tt("dve", v_[5][:], v_[0][:], tsq, ALU.mult, [vk[0], "T"], [vk[5]])
                        tt("dve", Xim[:, pq, :], v_[4][:], v_[5][:], ALU.add, [vk[4], vk[5]], [("Xim", pq)])
                    return f

                def outputs(ct):
                    for tp in range(8):
                        bkk = 4 + tp // 2
                        osl = slice((tp % 2) * 256, (tp % 2) * 256 + 256)
                        osl1 = slice((tp % 2) * 256 + 1, (tp % 2) * 256 + 256)
                        mms = [(Kt[:, tp - s_, :], up[ct][:, s_, :], False) for s_ in range(tp, -1, -1)]
                        for pq in range(4):
                            mms.append((W2[0][:, tp + 1, pq, :], Xre[:, pq, 0:255], True))
                            mms.append((W2[1][:, tp + 1, pq, :], Xim[:, pq, 0:255], True))
                        for n, (lh, rh, sh) in enumerate(mms):
                            P.op("pe", lambda h, lh=lh, rh=rh, sh=sh, n=n, bkk=bkk, osl=osl, osl1=osl1, nm=len(mms): h.matmul(
                                ps[bkk][:, osl1 if sh else osl], lh, rh, start=(n == 0), stop=(n == nm - 1)),
                                reads=["Kt", ("up", ct), ("W2", 0), ("W2", 1)] + [("Xre", q_) for q_ in range(4)] + [("Xim", q_) for q_ in range(4)], writes=[PK[bkk]], signal=(n == len(mms) - 1))
                        if tp % 2 == 1:
                            yb = ps[bkk][:]
                            if ("y%d" % ct) in dbg_d:
                                P.op("act", lambda h, yb=yb, tp=tp: h.copy(dump_sb[:, (tp // 2) * 512:(tp // 2 + 1) * 512], yb), reads=[PK[bkk]], writes=["dump_sb"])
                                if tp == 7:
                                    P.dma("sp", dbgb[0], dbg_d["y%d" % ct], dump_sb[:], reads=["dump_sb"], writes=["DBGOUT"])
                            V("act", lambda h, yb=yb: h.activation(g1[:], yb, AF.Square), [PK[bkk]], ["g1"])
                            V("dve", lambda h: h.tensor_scalar(g1[:], g1[:], 0.044715, 1.0, ALU.mult, ALU.add), ["g1"], ["g1"])
                            tt("dve", g2[:], yb, g1[:], ALU.mult, [PK[bkk], "g1"], ["g2"])
                            V("act", lambda h: h.activation(g2[:], g2[:], AF.Sigmoid, scale=1.5957691216057308), ["g2"], ["g2"])
                            ygv = yg[ct][:].rearrange("p (j t) -> p t j", t=8)[:, tp - 1:tp + 1, :]
                            tt("dve", ygv, yb.rearrange("p (t j) -> p t j", t=2), g2[:].rearrange("p (t j) -> p t j", t=2), ALU.mult,
                               [PK[bkk], "g2"], [("yg", ct)])
                    dump("yg%d" % ct, yg[ct][:], ("yg", ct))

                HEAD = 8
                load_pb(0)
                cB, cA = chains(0)
                h0 = len(cB) // 3
                run_merged([record(blkA1), cB[:h0]])
                run_merged([record(blkPA), record(blkT), cB[h0:]])
                exp_W1()
                P.barrier()
                if dbg:
                    dbgb[0] = P.new_batch(ordered=True)
                run_merged([cA])
                exp_Bbd()
                exp_W2()
                load_pb(1)
                nxt = chains(1)
                run_merged([nxt[0][:HEAD]])
                for ct in range(4):
                    if ct + 1 < 4:
                        cB, cA = nxt[0][HEAD:], nxt[1]
                    else:
                        cB, cA = [], []
                    hb = len(cB) // 2
                    pre = min(40, hb)
                    dx_matmuls(ct, 0)
                    dx_matmuls(ct, 1)
                    run_merged([cB[:pre]])
                    run_merged([record(slab(ct, 0)), record(slab(ct, 1)), cB[pre:hb], cA])
                    toeplitz(ct)
                    if ct + 1 < 4:
                        exp_Bbd()
                    dx_matmuls(ct, 2)
                    dx_matmuls(ct, 3)
                    run_merged([record(slab(ct, 2)), record(slab(ct, 3)), cB[hb:]])
                    if ct == 0:
                        dump("Xre", Xre[:].rearrange("p a b -> p (a b)"), ("Xre", 3), cols=1024)
                    outputs(ct)
                    if ct + 2 < 4:
                        load_pb(ct + 2)
                        nxt = chains(ct + 2)
                        run_merged([nxt[0][:HEAD]])
                    if ct + 1 < 4:
                        exp_W1()
                        exp_W2()
                p3a.close()
                P.barrier()
                if dbg:
                    dbgb[0] = P.new_batch(ordered=True)
                wos = [sbt(p3, "wos%d" % i, [128, 8, 512]) for i in range(2)]
                wo = sbt(p3, "wo", [128, 8, D], BF16)
                fg = sbt(p3, "fg", [128, D])
                xs4 = [sbt(p3, "xs4_%d" % i, [128, D]) for i in range(2)]
                yb_ = [sbt(p3, "yb%d" % i, [128, D]) for i in range(2)]
                ob_ = [sbt(p3, "ob%d" % i, [128, D]) for i in range(2)]
                junk4 = sbt(p3, "junk4", [128, D], BF16)
                ss2 = sbt(p3, "ss2", [128, 16])
                r2 = sbt(p3, "r2", [128, 16])
                bf_ = P.new_batch()
                P.dma("sp", bf_, fg[:], fg_d.partition_broadcast(128), writes=["fg"])
                P.op("dve", lambda h: h.memset(ss2[:], 0.0), writes=[("ss2", i_) for i_ in range(16)])
                for n in range(2):
                    bwn = P.new_batch()
                    P.dma("pool", bwn, wos[n][:], wout_d[:, n * 512:(n + 1) * 512].rearrange("(kt p) c -> p kt c", p=128), writes=[("wos", n)])
                wgs = sbt(p3, "wgs", [128, 4, 512])
                wgb = sbt(p3, "wgb", [128, 4, 512], BF16)
                bg = P.new_batch()
                P.dma("sp", bg, wgs[:], wglu_d.rearrange("(ct p) e -> p ct e", p=128), writes=["wgs"])
                P.op("act", lambda h: h.copy(wgb[:], wgs[:]), reads=["wgs"], writes=["wgb"])
                for et in range(4):
                    for tch in range(4):
                        tsl = slice(tch * 512, (tch + 1) * 512)
                        bkk = (et * 4 + tch) % 4
                        for ct in range(4):
                            P.op("pe", lambda h, et=et, ct=ct, tsl=tsl, bkk=bkk: h.matmul(
                                ps[bkk][:], wgb[:, ct, et * 128:(et + 1) * 128], yg[ct][:, tsl], start=(ct == 0), stop=(ct == 3)),
                                reads=["wgb", ("yg", ct)], writes=[PK[bkk]], signal=(ct == 3))
                        gk = "g1" if tch % 2 == 0 else "g2"
                        gt_ = g1 if tch % 2 == 0 else g2
                        V("act", lambda h, et=et, bkk=bkk, gt_=gt_: h.activation(gt_[:], ps[bkk][:], AF.Sigmoid, bias=bglu[:, et:et + 1]),
                          [PK[bkk], "cst"], [gk])
                        tt("dve", gt_[:], gt_[:], yg[et][:, tsl], ALU.mult, [gk, ("yg", et)], [gk])
                        tt("dve", zs[et][:, tsl], zs[et][:, tsl], gt_[:], ALU.mult, [gk, ("zs", et)], [("zs", et)])
                for i in range(4):
                    dump("os%d" % i, zs[i][:], ("zs", i))
                if upto == "p3":
                    P.stopped = True
                for n in range(2):
                    P.op("act", lambda h, n=n: h.copy(wo[:, :, n * 512:(n + 1) * 512], wos[n][:]), reads=[("wos", n)], writes=[("wo", n)])
                mix = za + zs
                bxs = [P.new_batch() for _ in range(16)]
                bout = [P.new_batch(ordered=True) for _ in range(2)]
                def stage_a(i):
                    s2 = i % 2
                    P.dma("pool", bxs[i], xs4[s2][:], x_d[i * 128:(i + 1) * 128, :], writes=[("xs", s2)])
                    for n in range(2):
                        bkk = 2 * s2 + n
                        for kt in range(8):
                            P.op("pe", lambda h, n=n, kt=kt, bkk=bkk: h.matmul(
                                ps[bkk][:], mix[kt][:, i * 128:(i + 1) * 128], wo[:, kt, n * 512:(n + 1) * 512], start=(kt == 0), stop=(kt == 7)),
                                reads=[("wo", n), (("za", kt) if kt < 4 else ("zs", kt - 4))], writes=[PK[bkk]], signal=(kt == 7))
                        P.op("dve", lambda h, n=n, bkk=bkk: h.tensor_tensor(yb_[s2][:, n * 512:(n + 1) * 512], ps[bkk][:], xs4[s2][:, n * 512:(n + 1) * 512], ALU.add),
                             reads=[PK[bkk], ("xs", s2)], writes=[("yb", s2)])
                    P.op("act", lambda h: h.activation(junk4[:], yb_[s2][:], AF.Square, accum_out=ss2[:, i:i + 1]),
                         reads=[("yb", s2)], writes=["junk", ("ss2", i)])
                    P.op("dve", lambda h: h.tensor_scalar(r2[:, i:i + 1], ss2[:, i:i + 1], 1.0 / D, EPS, ALU.mult, ALU.add),
                         reads=[("ss2", i)], writes=[("r2", i)])
                    P.op("act", lambda h: h.activation(r2[:, i:i + 1], r2[:, i:i + 1], AF.Sqrt), reads=[("r2", i)], writes=[("r2", i)])

                def stage_b(i):
                    s2 = i % 2
                    P.op("dve", lambda h: h.reciprocal(r2[:, i:i + 1], r2[:, i:i + 1]), reads=[("r2", i)], writes=[("r2", i)])
                    P.op("dve", lambda h: h.scalar_tensor_tensor(ob_[s2][:], yb_[s2][:], r2[:, i:i + 1], fg[:], ALU.mult, ALU.mult),
                         reads=[("yb", s2), ("r2", i), "fg"], writes=[("ob", s2)])
                    P.dma("sp", bout[s2], out_d[i * 128:(i + 1) * 128, :], ob_[s2][:], reads=[("ob", s2)], writes=["OUT"])

                for i in range(17):
                    if i < 16:
                        stage_a(i)
                    if i >= 1:
                        stage_b(i - 1)
                P.barrier()
                if dbg:
                    dbgb[0] = P.new_batch(ordered=True)
        except _Stop:
            qkv.close()
            P.barrier()
        P.emit()
    return nc


def _host_consts():
    cst = np.zeros((128, NCST), np.float32)
    cst[:, C_ID:C_ID + 128] = np.eye(128, dtype=np.float32)
    k = np.arange(128)
    cst[:, C_TRI:C_TRI + 128] = (k[None, :] >= k[:, None]).astype(np.float32)
    cst[:, C_BM:C_BM + 8] = (k[:, None] // 16 == np.arange(8)[None, :]).astype(np.float32)
    gv = np.zeros((8, 8), np.float32)
    gm = np.zeros((8, 8), np.float32)
    for a in range(8):
        own = (8 + a) // 2
        for b in range(8):
            if b < own:
                gv[a, b] = 1.0
            elif b == own:
                gm[a, b] = 1e9
            else:
                gm[a, b] = -1e9
    cst[:, C_GV:C_GV + 64] = gv.reshape(1, 64)
    cst[:, C_GM:C_GM + 64] = gm.reshape(1, 64)
    pm = np.zeros((128, 128), np.float32)
    for m in range(128):
        pm[64 * (m // 64) + ((m % 64) + 32) % 64, m] = 1.0
    cst[:, C_PERM:C_PERM + 128] = pm
    half = 32
    inv = (1.0 / (10000.0 ** (np.arange(half, dtype=np.float32) / half))).astype(np.float32)
    ang = (np.arange(L, dtype=np.float32)[:, None] * inv[None, :]).astype(np.float32).astype(np.float64)
    cos, sin = np.cos(ang).T, np.sin(ang).T
    c64 = np.concatenate([cos, cos], 0)
    s64 = np.concatenate([-sin, sin], 0)
    rope = np.stack([np.concatenate([c64, c64], 0), np.concatenate([s64, s64], 0)], 1).astype(np.float32)
    onehot = (np.arange(L)[None, :] // 256 == np.arange(8)[:, None]).astype(np.float32)
    return cst, rope, onehot


def _prep_shared(norm_gain, w_in, w_out, lam_re, lam_im, b_re, b_im, c_re, c_im, d_skip, log_dt, w_glu, b_glu, final_gain):
    cst, rope, onehot = _host_consts()
    cst[:, C_G:C_G + 8] = norm_gain[0].reshape(8, 128).T
    cst[:, C_BG:C_BG + 4] = b_glu[0].reshape(4, 128).T
    W = w_in[0]
    A = 512
    q, k_, v, zatt, u, zss = W[:, 0:A], W[:, A:2 * A], W[:, 2 * A:3 * A], W[:, 3 * A:4 * A], W[:, 4 * A:5 * A], W[:, 5 * A:6 * A]
    chunks = [q[:, 0:256], q[:, 256:512], k_[:, 0:256], k_[:, 256:512]]
    for m in (zatt, u, zss, v):
        chunks.append(m[:, 0:256])
        chunks.append(m[:, 256:512])
    w_in_c = np.ascontiguousarray(np.stack(chunks, 0)).astype(np.float32)
    pA = np.zeros((4, 128, NPA), np.float32)
    pB = np.zeros((4, 128, NPB), np.float32)
    lre, lim, ldt = lam_re[0], lam_im[0], log_dt[0]
    bre, bim, cre, cim, dsk = b_re[0], b_im[0], c_re[0], c_im[0], d_skip[0]
    for ct in range(4):
        g = slice(8 * ct, 8 * ct + 8)

        def A_(m):
            return m.reshape(8, 4, 16).transpose(0, 2, 1).reshape(128, 4)
        pA[ct, :, 0:4] = A_(lre[g])
        pA[ct, :, 4:8] = A_(lim[g])
        pA[ct, :, 8] = np.repeat(ldt[g], 16)
        pA[ct, :, 9:73] = cre[g].reshape(8, 16, 4, 16).transpose(0, 3, 2, 1).reshape(128, 64)
        pA[ct, :, 73:137] = cim[g].reshape(8, 16, 4, 16).transpose(0, 3, 2, 1).reshape(128, 64)
        pA[ct, :, 137:201] = bre[g].reshape(8, 4, 16, 16).transpose(0, 2, 1, 3).reshape(128, 64)
        pA[ct, :, 201:265] = bim[g].reshape(8, 4, 16, 16).transpose(0, 2, 1, 3).reshape(128, 64)
        pB[ct, :, 0:64] = np.repeat(lre[g], 16, axis=0)
        pB[ct, :, 64:128] = np.repeat(lim[g], 16, axis=0)
        pB[ct, :, 128] = np.repeat(ldt[g], 16)
        pB[ct, :, 129:193] = bre[g].transpose(0, 2, 1).reshape(128, 64)
        pB[ct, :, 193:257] = bim[g].transpose(0, 2, 1).reshape(128, 64)
        pB[ct, :, 257] = dsk[g].reshape(128)
    return dict(w_in_c=w_in_c, w_out=np.ascontiguousarray(w_out[0]), w_glu=np.ascontiguousarray(w_glu[0]), cst=cst, rope=rope,
                fgain=np.ascontiguousarray(final_gain.reshape(1, D)), onehot=onehot, pA=pA, pB=pB)


def kernel(x, norm_gain, w_in, w_out, lam_re, lam_im, b_re, b_im, c_re, c_im, d_skip, log_dt, w_glu, b_glu, final_gain):
    args = [np.asarray(a, dtype=np.float32) for a in (norm_gain, w_in, w_out, lam_re, lam_im, b_re, b_im, c_re, c_im,
                                                      d_skip, log_dt, w_glu, b_glu, final_gain)]
    x = np.asarray(x, dtype=np.float32)
    shared = _prep_shared(*args)
    nc = build()
    in_maps = []
    for b in range(8):
        m = dict(shared)
        m["x"] = np.ascontiguousarray(x[b])
        in_maps.append(m)
    res = run_bass_kernel_spmd(nc, in_maps, core_ids=list(range(8)))
    return np.stack([np.asarray(r["out"], dtype=np.float32) for r in res.results], 0)
```
